# Optimizing a Trainium2 kernel written in Bass

```python
import math
import jax, jax.numpy as jnp
from jax import lax
import numpy as np

D_MODEL = 2048
BATCH = 32
SEQ = 256
DEPTH = 4
DEC_BATCH = 2
DEC_SEQ = 1024
PAST_LEN = 256

GRID_W = 64
MIX_W = D_MODEL
MLA_HEADS = 8
MLA_NOPE = 128
MLA_ROPE = 64
MLA_QK = MLA_NOPE + MLA_ROPE
MLA_V = 128
MLA_Q_RANK = 768
MLA_KV_RANK = 512
ROPE_F = MLA_ROPE // 4
ROPE_BASE = 10000.0
Q_BLOCK = 128
GDN_HEADS = 4
GDN_DK = 128
GDN_DV = 128
GDN_CHUNK = 64
SSD_HEADS = 8
SSD_P = 64
SSD_GROUPS = 2
SSD_N = 128
SSD_CHUNK = 128
SSD_INNER = SSD_HEADS * SSD_P
CONV_W = 5
FF = -(-8 * D_MODEL // (3 * 256)) * 256
MLA_IN = MLA_Q_RANK + MLA_KV_RANK + MLA_ROPE
GDN_QK = GDN_HEADS * GDN_DK
GDN_VW = GDN_HEADS * GDN_DV
GDN_IN = 2 * GDN_QK + 2 * GDN_VW + 4 * GDN_HEADS
SSD_XBC = SSD_INNER + 2 * SSD_GROUPS * SSD_N
SSD_IN = SSD_INNER + SSD_XBC + 2 * SSD_HEADS
IN_W = MLA_IN + GDN_IN + SSD_IN
EPS = 1e-6

kernel_name = "hybrid_mla_gdn_ssd_diffusion_step"

f32 = jnp.float32


def rmsnorm(x, g):
    xf = x.astype(f32)
    y = xf * lax.rsqrt(jnp.mean(xf * xf, -1, keepdims=True) + EPS)
    return (y * g.astype(f32)).astype(x.dtype)


def l2norm(x):
    xf = x.astype(f32)
    return (xf * lax.rsqrt(jnp.sum(xf * xf, -1, keepdims=True) + EPS)).astype(x.dtype)


def dwconv_centred(x, w):
    pad = (w.shape[0] - 1) // 2
    return lax.conv_general_dilated(x, w[:, None, :].astype(x.dtype), (1,), [(pad, pad)],
                                    dimension_numbers=('NWC', 'WIO', 'NWC'),
                                    feature_group_count=x.shape[-1])


def axial_rope_tables(n_tokens):
    rows = n_tokens // GRID_W
    row = jnp.repeat(jnp.arange(rows, dtype=f32), GRID_W)
    col = jnp.tile(jnp.arange(GRID_W, dtype=f32), rows)
    inv = ROPE_BASE ** (-jnp.arange(ROPE_F, dtype=f32) / ROPE_F)
    ang = jnp.stack([row[:, None] * inv, col[:, None] * inv], axis=1)
    return jnp.cos(ang), jnp.sin(ang)


def rope2d(u, cos, sin):
    sh = u.shape
    uf = u.astype(f32).reshape(sh[:-1] + (2, 2, ROPE_F))
    c, s = cos[:, None], sin[:, None]
    u1, u2 = uf[..., 0, :], uf[..., 1, :]
    out = jnp.stack([u1 * c - u2 * s, u2 * c + u1 * s], axis=-2)
    return out.reshape(sh).astype(u.dtype)


def rope_tail(t, rope):
    cos, sin = rope
    return jnp.concatenate([t[..., :MLA_NOPE], rope2d(t[..., MLA_NOPE:], cos, sin)], -1)


def softmax_attention(q, k, v):
    B, Tq, H, D = q.shape
    nb = Tq // Q_BLOCK
    scale = D ** -0.5
    qb = q.reshape(B, nb, Q_BLOCK, H, D).swapaxes(0, 1)

    def block(qi):
        s = jnp.einsum('bqhd,bkhd->bhqk', qi, k).astype(f32) * scale
        pr = jax.nn.softmax(s, axis=-1).astype(v.dtype)
        return jnp.einsum('bhqk,bkhd->bqhd', pr, v)

    o = lax.map(block, qb)
    return o.swapaxes(0, 1).reshape(B, Tq, H, v.shape[-1])


def mla_keys_values(ckv, krope, p, l):
    B, T, _ = ckv.shape
    kv = (ckv @ p['mla_w_ukv'][l]).reshape(B, T, MLA_HEADS, MLA_NOPE + MLA_V)
    k_nope, v = kv[..., :MLA_NOPE], kv[..., MLA_NOPE:]
    k_pe = jnp.broadcast_to(krope[:, :, None, :], (B, T, MLA_HEADS, MLA_ROPE)).astype(k_nope.dtype)
    k = jnp.concatenate([k_nope, k_pe], -1)
    return rmsnorm(k, p['mla_k_g'][l]), v


def mla(u, p, l, rope, ctx_ckv, ctx_krope):
    B, T, _ = u.shape
    cq, ckv, krope = jnp.split(u, [MLA_Q_RANK, MLA_Q_RANK + MLA_KV_RANK], -1)
    ckv = rmsnorm(ckv, p['mla_kvnorm_g'][l])
    q = (rmsnorm(cq, p['mla_qnorm_g'][l]) @ p['mla_w_uq'][l]).reshape(B, T, MLA_HEADS, MLA_QK)
    q = rmsnorm(q, p['mla_q_g'][l])
    k, v = mla_keys_values(ckv, krope, p, l)
    if rope is not None:
        q = rope_tail(q, rope)
        k = rope_tail(k, rope)
    if ctx_ckv is not None:
        kc, vc = mla_keys_values(ctx_ckv.astype(u.dtype), ctx_krope.astype(u.dtype), p, l)
        k = jnp.concatenate([kc, k], 1)
        v = jnp.concatenate([vc, v], 1)
    o = softmax_attention(q, k, v)
    return o.reshape(B, T, MLA_HEADS * MLA_V), ckv, krope


def delta_rule_chunked(q, k, v, g, beta, s0):
    B, T, H, DK = q.shape
    DV = v.shape[-1]
    C = GDN_CHUNK
    N = T // C
    ch4 = lambda t: t.astype(f32).reshape(B, N, C, H, -1).transpose(1, 0, 3, 2, 4)
    ch3 = lambda t: t.astype(f32).reshape(B, N, C, H).transpose(1, 0, 3, 2)
    q = ch4(q) * (DK ** -0.5)
    k = ch4(k)
    v = ch4(v)
    beta = ch3(beta)
    gc = jnp.cumsum(ch3(g), -1)
    tri = jnp.tril(jnp.ones((C, C), bool))
    strict = jnp.tril(jnp.ones((C, C), bool), -1)
    decay = jnp.exp(jnp.where(tri, gc[..., :, None] - gc[..., None, :], -jnp.inf))
    kb = k * beta[..., None]
    A = jnp.where(strict, jnp.einsum('nbhid,nbhjd->nbhij', kb, k) * decay, 0.0)
    eye = jnp.eye(C, dtype=f32)
    Tm = lax.linalg.triangular_solve(A + eye, jnp.broadcast_to(eye, A.shape), left_side=True,
                                     lower=True, unit_diagonal=True)
    w = Tm @ (kb * jnp.exp(gc)[..., None])
    uu = Tm @ (v * beta[..., None])
    qk = jnp.where(tri, jnp.einsum('nbhid,nbhjd->nbhij', q, k) * decay, 0.0)

    def step(S, inp):
        qi, ki, ui, wi, gi, ai = inp
        v_new = ui - wi @ S
        o = (qi * jnp.exp(gi)[..., None]) @ S + ai @ v_new
        gl = gi[..., -1]
        S = S * jnp.exp(gl)[..., None, None] + jnp.einsum(
            'bhcd,bhce->bhde', ki * jnp.exp(gl[..., None] - gi)[..., None], v_new)
        return S, o

    s_fin, o = lax.scan(step, s0.astype(f32), (q, k, uu, w, gc, qk))
    return o.transpose(1, 0, 3, 2, 4).reshape(B, T, H, DV), s_fin


def gated_deltanet(u, p, l, s0):
    B, T, _ = u.shape
    qkv, z, b, a = jnp.split(u, [2 * GDN_QK + GDN_VW, 2 * GDN_QK + 2 * GDN_VW,
                                 2 * GDN_QK + 2 * GDN_VW + 2 * GDN_HEADS], -1)
    qkv = jax.nn.silu(dwconv_centred(qkv, p['gdn_conv_w'][l]))
    q, k, v = jnp.split(qkv, [GDN_QK, 2 * GDN_QK], -1)
    q = l2norm(q.reshape(B, T, GDN_HEADS, GDN_DK))
    k = l2norm(k.reshape(B, T, GDN_HEADS, GDN_DK))
    v = v.reshape(B, T, GDN_HEADS, GDN_DV)
    beta = jax.nn.sigmoid(b.astype(f32)).reshape(B, T, 2, GDN_HEADS)
    g = -jnp.exp(p['gdn_a_log'][l].astype(f32)) * jax.nn.softplus(
        a.astype(f32).reshape(B, T, 2, GDN_HEADS) + p['gdn_dt_bias'][l].astype(f32))
    if s0 is None:
        s0 = jnp.zeros((B, 2, GDN_HEADS, GDN_DK, GDN_DV), f32)
    flip = lambda t: jnp.flip(t, 1)
    o_f, s_f = delta_rule_chunked(q, k, v, g[:, :, 0], beta[:, :, 0], s0[:, 0])
    o_b, s_b = delta_rule_chunked(flip(q), flip(k), flip(v), flip(g[:, :, 1]), flip(beta[:, :, 1]), s0[:, 1])
    o = o_f + flip(o_b)
    o = rmsnorm(o, p['gdn_norm_g'][l]) * jax.nn.silu(z.astype(f32).reshape(B, T, GDN_HEADS, GDN_DV))
    return o.reshape(B, T, GDN_VW).astype(u.dtype), jnp.stack([s_f, s_b], 1).astype(u.dtype)


def ssd_chunked(x, dt, A, Bm, Cm, s0):
    B, T, H, P = x.shape
    Q = SSD_CHUNK
    nc = T // Q
    rep = H // Bm.shape[2]
    Bh = jnp.repeat(Bm.astype(f32), rep, 2).reshape(B, nc, Q, H, -1)
    Ch = jnp.repeat(Cm.astype(f32), rep, 2).reshape(B, nc, Q, H, -1)
    xdt = (x.astype(f32) * dt[..., None]).reshape(B, nc, Q, H, P)
    acum = jnp.cumsum((dt * A).reshape(B, nc, Q, H).transpose(0, 3, 1, 2), -1)
    tri = jnp.tril(jnp.ones((Q, Q), bool))
    Lmat = jnp.exp(jnp.where(tri, acum[..., :, None] - acum[..., None, :], -jnp.inf))
    scores = jnp.einsum('bclhn,bcshn->bhcls', Ch, Bh) * Lmat
    y = jnp.einsum('bhcls,bcshp->bclhp', scores, xdt)
    decay_to_end = jnp.exp(acum[..., -1:] - acum)
    chunk_states = jnp.einsum('bcshn,bhcs,bcshp->cbhpn', Bh, decay_to_end, xdt)
    chunk_decay = jnp.exp(acum[..., -1]).transpose(2, 0, 1)

    def step(S, inp):
        st, dec = inp
        return S * dec[..., None, None] + st, S

    s_fin, s_start = lax.scan(step, s0.astype(f32), (chunk_states, chunk_decay))
    y = y + jnp.einsum('bclhn,cbhpn,bhcl->bclhp', Ch, s_start, jnp.exp(acum))
    return y.reshape(B, T, H, P), s_fin


def mamba2_ssd(u, p, l, s0):
    B, T, _ = u.shape
    z, xbc, dt2 = jnp.split(u, [SSD_INNER, SSD_INNER + SSD_XBC], -1)
    xbc = jax.nn.silu(dwconv_centred(xbc, p['ssd_conv_w'][l]) + p['ssd_conv_b'][l])
    x, Bm, Cm = jnp.split(xbc, [SSD_INNER, SSD_INNER + SSD_GROUPS * SSD_N], -1)
    x = x.reshape(B, T, SSD_HEADS, SSD_P)
    Bm = Bm.reshape(B, T, SSD_GROUPS, SSD_N)
    Cm = Cm.reshape(B, T, SSD_GROUPS, SSD_N)
    dt = jax.nn.softplus(dt2.astype(f32).reshape(B, T, 2, SSD_HEADS) + p['ssd_dt_bias'][l].astype(f32))
    A = -jnp.exp(p['ssd_a_log'][l].astype(f32))
    if s0 is None:
        s0 = jnp.zeros((B, 2, SSD_HEADS, SSD_P, SSD_N), f32)
    flip = lambda t: jnp.flip(t, 1)
    y_f, s_f = ssd_chunked(x, dt[:, :, 0], A[0], Bm, Cm, s0[:, 0])
    y_b, s_b = ssd_chunked(flip(x), flip(dt[:, :, 1]), A[1], flip(Bm), flip(Cm), s0[:, 1])
    y = y_f + flip(y_b) + x.astype(f32) * p['ssd_d'][l].astype(f32)[:, None]
    y = y.reshape(B, T, SSD_INNER) * jax.nn.silu(z.astype(f32))
    y = rmsnorm(y.reshape(B, T, SSD_GROUPS, SSD_INNER // SSD_GROUPS),
                p['ssd_norm_g'][l].reshape(SSD_GROUPS, SSD_INNER // SSD_GROUPS))
    return y.reshape(B, T, SSD_INNER).astype(u.dtype), jnp.stack([s_f, s_b], 1).astype(u.dtype)


def trunk_layer(x, cvec, p, l, rope, ctx):
    mods = (jax.nn.silu(cvec) @ p['ada_w'][l] + p['ada_b'][l]).reshape(cvec.shape[0], 6, 1, D_MODEL)
    shift1, scale1, gate1, shift2, scale2, gate2 = mods.transpose(1, 0, 2, 3)
    h = rmsnorm(x, p['norm1_g'][l]) * (1 + scale1) + shift1
    u = h @ p['w_in'][l]
    u_mla, u_gdn, u_ssd = jnp.split(u, [MLA_IN, MLA_IN + GDN_IN], -1)
    ctx_ckv, ctx_kr, s_gdn0, s_ssd0 = ctx if ctx is not None else (None, None, None, None)
    o_mla, ckv, kr = mla(u_mla, p, l, rope, ctx_ckv, ctx_kr)
    o_gdn, s_gdn = gated_deltanet(u_gdn, p, l, s_gdn0)
    o_ssd, s_ssd = mamba2_ssd(u_ssd, p, l, s_ssd0)
    mixed = jnp.concatenate([o_mla, o_gdn, o_ssd], -1) @ p['w_out'][l]
    x = x + gate1 * mixed
    h = rmsnorm(x, p['norm2_g'][l]) * (1 + scale2) + shift2
    gt, up = jnp.split(h @ p['ffn_w_gu'][l], 2, -1)
    x = x + gate2 * ((jax.nn.silu(gt) * up) @ p['ffn_w_down'][l])
    return x, (ckv, kr, s_gdn, s_ssd)


def setup_inputs(seed: int = 0) -> dict:
    key = jax.random.key(seed)
    ks = iter(jax.random.split(key, 40))
    nrm = lambda shape, s: jax.random.normal(next(ks), shape, f32) * s
    gain = lambda shape: 1.0 + 0.02 * jax.random.normal(next(ks), shape, f32)

    def dt_bias(shape):
        dt = jnp.exp(jax.random.uniform(next(ks), shape, f32, math.log(1e-3), math.log(1e-1)))
        return dt + jnp.log(-jnp.expm1(-dt))

    a_log = lambda shape: jnp.log(jax.random.uniform(next(ks), shape, f32, 1.0, 16.0))
    L = DEPTH
    return {
        'x_prompt': nrm((BATCH, SEQ, D_MODEL), 1.0),
        'x_sample': nrm((DEC_BATCH, DEC_SEQ, D_MODEL), 1.0),
        'cache_mla_ckv': nrm((DEC_BATCH, L, PAST_LEN, MLA_KV_RANK), 1.0),
        'cache_mla_krope': nrm((DEC_BATCH, L, PAST_LEN, MLA_ROPE), 1.0),
        'state_gdn': nrm((DEC_BATCH, L, 2, GDN_HEADS, GDN_DK, GDN_DV), 0.1),
        'state_ssd': nrm((DEC_BATCH, L, 2, SSD_HEADS, SSD_P, SSD_N), 0.1),
        'c': nrm((DEC_BATCH, D_MODEL), 1.0),
        'c_ctx': nrm((D_MODEL,), 1.0),
        'norm1_g': gain((L, D_MODEL)),
        'norm2_g': gain((L, D_MODEL)),
        'ada_w': nrm((L, D_MODEL, 6 * D_MODEL), 0.5 * D_MODEL ** -0.5),
        'ada_b': nrm((L, 6 * D_MODEL), 0.01),
        'w_in': nrm((L, D_MODEL, IN_W), D_MODEL ** -0.5),
        'w_out': nrm((L, MIX_W, D_MODEL), MIX_W ** -0.5),
        'mla_qnorm_g': gain((L, MLA_Q_RANK)),
        'mla_w_uq': nrm((L, MLA_Q_RANK, MLA_HEADS * MLA_QK), MLA_Q_RANK ** -0.5),
        'mla_kvnorm_g': gain((L, MLA_KV_RANK)),
        'mla_w_ukv': nrm((L, MLA_KV_RANK, MLA_HEADS * (MLA_NOPE + MLA_V)), MLA_KV_RANK ** -0.5),
        'mla_q_g': gain((L, MLA_QK)),
        'mla_k_g': gain((L, MLA_QK)),
        'gdn_conv_w': nrm((L, CONV_W, 2 * GDN_QK + GDN_VW), CONV_W ** -0.5),
        'gdn_a_log': a_log((L, 2, GDN_HEADS)),
        'gdn_dt_bias': dt_bias((L, 2, GDN_HEADS)),
        'gdn_norm_g': gain((L, GDN_DV)),
        'ssd_conv_w': nrm((L, CONV_W, SSD_XBC), CONV_W ** -0.5),
        'ssd_conv_b': nrm((L, SSD_XBC), 0.01),
        'ssd_a_log': a_log((L, 2, SSD_HEADS)),
        'ssd_dt_bias': dt_bias((L, 2, SSD_HEADS)),
        'ssd_d': gain((L, SSD_HEADS)),
        'ssd_norm_g': gain((L, SSD_INNER)),
        'ffn_w_gu': nrm((L, D_MODEL, 2 * FF), D_MODEL ** -0.5),
        'ffn_w_down': nrm((L, FF, D_MODEL), FF ** -0.5),
    }


def reference(x_prompt, x_sample, cache_mla_ckv, cache_mla_krope, state_gdn, state_ssd, c, c_ctx,
              norm1_g, norm2_g, ada_w, ada_b, w_in, w_out, mla_qnorm_g, mla_w_uq, mla_kvnorm_g,
              mla_w_ukv, mla_q_g, mla_k_g, gdn_conv_w, gdn_a_log, gdn_dt_bias, gdn_norm_g,
              ssd_conv_w, ssd_conv_b, ssd_a_log, ssd_dt_bias, ssd_d, ssd_norm_g, ffn_w_gu, ffn_w_down):
    p = dict(norm1_g=norm1_g, norm2_g=norm2_g, ada_w=ada_w, ada_b=ada_b, w_in=w_in, w_out=w_out,
             mla_qnorm_g=mla_qnorm_g, mla_w_uq=mla_w_uq, mla_kvnorm_g=mla_kvnorm_g,
             mla_w_ukv=mla_w_ukv, mla_q_g=mla_q_g, mla_k_g=mla_k_g, gdn_conv_w=gdn_conv_w,
             gdn_a_log=gdn_a_log, gdn_dt_bias=gdn_dt_bias, gdn_norm_g=gdn_norm_g,
             ssd_conv_w=ssd_conv_w, ssd_conv_b=ssd_conv_b, ssd_a_log=ssd_a_log,
             ssd_dt_bias=ssd_dt_bias, ssd_d=ssd_d, ssd_norm_g=ssd_norm_g,
             ffn_w_gu=ffn_w_gu, ffn_w_down=ffn_w_down)
    xp = x_prompt
    cc = c_ctx[None, :]
    ckvs, krs, sgs, sss = [], [], [], []
    for l in range(DEPTH):
        xp, (ckv, kr, sg, ss) = trunk_layer(xp, cc, p, l, None, None)
        ckvs.append(ckv)
        krs.append(kr)
        sgs.append(sg)
        sss.append(ss)
    rope = axial_rope_tables(x_sample.shape[1])
    xs = x_sample
    for l in range(DEPTH):
        xs, _ = trunk_layer(xs, c, p, l, rope,
                            (cache_mla_ckv[:, l], cache_mla_krope[:, l], state_gdn[:, l], state_ssd[:, l]))
    return (xp, xs, jnp.stack(ckvs, 1), jnp.stack(krs, 1), jnp.stack(sgs, 1), jnp.stack(sss, 1))
```

```python
import numpy as np
import concourse.bass as bass
import concourse.mybir as mybir
from concourse.bass_utils import run_bass_kernel_spmd
from contextlib import ExitStack

F32 = mybir.dt.float32
BF16 = mybir.dt.bfloat16
AF = mybir.ActivationFunctionType
ALU = mybir.AluOpType

L = 4
D = 2048
KC = 16
NS = 5
SL = 256
T = NS * SL
NT = T // 128
TK = T + 256
FFH = 5632
FKC = 44
EPS = 1e-6
TG = [(0, 512), (512, 512), (1024, 256)]
ENGS = ("tensor", "vector", "scalar", "gpsimd", "sync")

OC_CQ, OC_CKV, OC_KR = 0, 6, 10
OC_GQ, OC_GK, OC_GV, OC_GZ = 11, 15, 19, 23
OC_SZ, OC_SX, OC_SB, OC_SC = 27, 31, 35, 37
N_OC_IN = 39

PC_N1, PC_N2, PC_ADAB, PC_QN, PC_KVN, PC_QG, PC_KG = 0, 16, 32, 128, 134, 138, 140
PC_GCW, PC_SCW, PC_SCB, PC_GNG, PC_SNG, PC_SD = 142, 202, 242, 250, 251, 255
PC_TOK = 259
NPC = PC_TOK + 48
TC_MASK, TC_AL, TC_AR, TC_AS = 0, 30, 38, 46
NTC = 56
C_ID, C_ONE, C_T128F, C_T128B, C_T64F, C_T64B, C_S64F, C_S64B, C_BLK, C_ROT = range(10)
NCONST = 10


class Tile:
    def __init__(self, handle, ncells=1, name="", excl=False):
        self.h = handle
        self.name = name
        self.ncells = ncells
        self.excl = excl
        self.state = [[None, {}] for _ in range(ncells)]

    def v(self, ap=None, cells=None):
        if ap is None:
            ap = self.h[:]
        if cells is None:
            cells = range(self.ncells)
        elif isinstance(cells, int):
            cells = (cells,)
        return View(self, ap, tuple(cells))

    def __getitem__(self, idx):
        return self.v(self.h[idx])


class View:
    def __init__(self, tile, ap, cells):
        self.tile = tile
        self.ap = ap
        self.cells = cells

    def __getitem__(self, idx):
        return View(self.tile, self.ap[idx], self.cells)


class Chan:
    def __init__(self, sem, name):
        self.sem = sem
        self.count = 0
        self.name = name


class Prog:
    def __init__(self, nc, es):
        self.nc = nc
        self.es = es
        self.ops = {e: [] for e in ENGS}
        self.sem = {e: es.enter_context(nc.semaphore("s_" + e)) for e in ENGS}
        self.cnt = {e: 0 for e in ENGS}
        self.seen = {e: {} for e in ENGS}
        self.chans = []
        self.rr = 0

    def sb(self, name, shape, dt, ncells=1):
        return Tile(self.es.enter_context(self.nc.sbuf_tensor("t_" + name, list(shape), dt)), ncells, name)

    def psum(self, name, shape, dt):
        return Tile(self.es.enter_context(self.nc.psum_tensor("p_" + name, list(shape), dt)), 1, name, excl=True)

    def chan(self, name):
        c = Chan(self.es.enter_context(self.nc.semaphore("c_" + name)), name)
        self.chans.append(c)
        return c

    def _collect(self, eng, reads, writes, is_dma):
        waits = {}

        def need(tok):
            if tok is None:
                return
            sem, val, teng, small = tok
            if teng == eng and not is_dma and not small:
                return
            k = id(sem)
            if self.seen[eng].get(k, 0) >= val:
                return
            if k not in waits or waits[k][1] < val:
                waits[k] = (sem, val)

        for v in reads:
            for c in v.cells:
                need(v.tile.state[c][0])
        for v in writes:
            for c in v.cells:
                st = v.tile.state[c]
                need(st[0])
                for t in st[1].values():
                    need(t)
        for k, (sem, val) in waits.items():
            self.seen[eng][k] = val
        return list(waits.values())

    def _split(self, reads, writes):
        r2, w2 = [], list(writes)
        for v in reads:
            (w2 if v.tile.excl else r2).append(v)
        return r2, w2

    def op(self, eng, fn, reads=(), writes=()):
        reads, writes = self._split(reads, writes)
        waits = self._collect(eng, reads, writes, False)
        self.cnt[eng] += 1
        small = False
        if eng != "tensor":
            for v in writes:
                n = 1
                for d_ in v.ap.shape[1:]:
                    n *= d_
                if n <= 128:
                    small = True
        tok = (self.sem[eng], self.cnt[eng], eng, small)
        self.ops[eng].append((waits, fn, (self.sem[eng], 1)))
        for v in writes:
            for c in v.cells:
                v.tile.state[c][0] = tok
                v.tile.state[c][1] = {}
        for v in reads:
            for c in v.cells:
                v.tile.state[c][1][eng] = tok
        return tok

    def dma(self, eng, chan, out, in_, out_view=None, in_view=None):
        reads = [in_view] if in_view is not None else []
        writes = [out_view] if out_view is not None else []
        waits = self._collect(eng, reads, writes, True)
        chan.count += 16
        tok = (chan.sem, chan.count, None, False)
        self.ops[eng].append((waits, lambda e: e.dma_start(out=out, in_=in_), (chan.sem, 16)))
        for v in writes:
            for c in v.cells:
                v.tile.state[c][0] = tok
                v.tile.state[c][1] = {}
        for v in reads:
            for c in v.cells:
                v.tile.state[c][1]["dma_" + chan.name] = tok
        return tok

    def finish(self):
        fin = [(c.sem, c.count) for c in self.chans if c.count > 0]
        self.ops["sync"].append((fin, None, None))
        with self.nc.Block() as block:
            def mk(ename):
                def body(e):
                    for waits, fn, inc in self.ops[ename]:
                        for sem, val in waits:
                            e.wait_ge(sem, val)
                        if fn is not None:
                            fn(e).then_inc(inc[0], inc[1])
                return body
            block.tensor(mk("tensor"))
            block.vector(mk("vector"))
            block.scalar(mk("scalar"))
            block.gpsimd(mk("gpsimd"))
            block.sync(mk("sync"))

    def mm(self, out, lhsT, rhs, start=True, stop=True):
        return self.op("tensor", lambda e: e.matmul(out.ap, lhsT.ap, rhs.ap, start=start, stop=stop),
                       reads=[lhsT, rhs], writes=[out])

    def transpose(self, out, in_, ident):
        return self.op("tensor", lambda e: e.transpose(out.ap, in_.ap, ident.ap), reads=[in_, ident], writes=[out])

    def act(self, out, in_, func, bias=None, scale=None):
        reads = [in_]
        kw = {}
        for nm, val in (("bias", bias), ("scale", scale)):
            if val is None:
                continue
            if isinstance(val, View):
                reads.append(val)
                kw[nm] = val.ap
            else:
                kw[nm] = val
        return self.op("scalar", lambda e: e.activation(out.ap, in_.ap, func, **kw), reads=reads, writes=[out])

    def ts(self, out, in0, s1, s2, op0, op1=None):
        reads = [in0]
        a1, a2 = s1, s2
        if isinstance(s1, View):
            reads.append(s1)
            a1 = s1.ap
        if isinstance(s2, View):
            reads.append(s2)
            a2 = s2.ap
        if op1 is None:
            return self.op("vector", lambda e: e.tensor_scalar(out.ap, in0.ap, a1, None, op0), reads=reads, writes=[out])
        return self.op("vector", lambda e: e.tensor_scalar(out.ap, in0.ap, a1, a2, op0, op1), reads=reads, writes=[out])

    def tt(self, out, in0, in1, op):
        return self.op("vector", lambda e: e.tensor_tensor(out.ap, in0.ap, in1.ap, op), reads=[in0, in1], writes=[out])

    def stt(self, out, in0, s, in1, op0, op1):
        reads = [in0, in1]
        a = s
        if isinstance(s, View):
            reads.append(s)
            a = s.ap
        return self.op("vector", lambda e: e.scalar_tensor_tensor(out.ap, in0.ap, a, in1.ap, op0, op1),
                       reads=reads, writes=[out])

    def copy(self, out, in_, eng=None):
        if eng is None:
            self.rr ^= 1
            eng = "scalar" if self.rr else "vector"
        if eng == "scalar":
            return self.op(eng, lambda e: e.copy(out.ap, in_.ap), reads=[in_], writes=[out])
        return self.op(eng, lambda e: e.tensor_copy(out.ap, in_.ap), reads=[in_], writes=[out])

    def ascale(self, out, in_, scale):
        return self.act(out, in_, AF.Identity, scale=scale)

    def memset(self, out, val):
        return self.op("vector", lambda e: e.memset(out.ap, val), reads=[], writes=[out])

    def recip(self, out, in_):
        return self.op("vector", lambda e: e.reciprocal(out.ap, in_.ap), reads=[in_], writes=[out])


def build_program(n_layers=None, debug=False):
    n_layers = L if n_layers is None else n_layers
    nc = bass.Bass("TRN2", target_bir_lowering=False)

    def din(name, shape):
        return nc.dram_tensor(name, list(shape), F32, kind="ExternalInput").ap()

    def dout(name, shape):
        return nc.dram_tensor(name, list(shape), F32, kind="ExternalOutput").ap()

    d_x = din("xT", [128, KC, T])
    d_cond = din("condT", [128, KC, NS])
    d_tab = din("tab", [128, NTC])
    d_ropec = din("ropeC", [64, T])
    d_ropes = din("ropeS", [64, T])
    d_const = din("consts", [128, NCONST, 128])
    d_pl = din("pl", [L, 128, NPC])
    d_ctxckv = din("ctx_ckv", [L, 128, 4, 256])
    d_ctxkr = din("ctx_kr", [L, 64, 256])
    d_ginit = din("gdn_init", [L, 2, 4, 128, 128])
    d_sinit = din("ssd_init", [L, 2, 8, 128, 64])
    d_ada = din("ada_w", [L, 96, 128, KC, 128])
    d_win = din("w_in", [L, N_OC_IN, 128, KC, 128])
    d_wsm = din("w_small", [L, 128, KC, 32])
    d_wuq = din("w_uq", [L, 8, 128, 6, 192])
    d_wukv = din("w_ukv", [L, 8, 128, 4, 256])
    d_wout = din("w_out", [L, 16, 128, KC, 128])
    d_wgu = din("w_gu", [L, 88, 128, KC, 128])
    d_wdn = din("w_dn", [L, 16, 128, FKC, 128])

    o_y = dout("yT", [128, KC, T])
    o_ckv = dout("o_ckv", [L, 128, 4, T])
    o_kr = dout("o_kr", [L, 64, T])
    o_gdn = dout("o_gdn", [L, NS, 2, 4, 128, 128])
    o_ssd = dout("o_ssd", [L, NS, 2, 8, 128, 64])
    o_dbg = nc.dram_tensor("o_dbg", [128, KC, T], BF16, kind="ExternalOutput").ap() if debug else None
    o_dbg2 = dout("o_dbg2", [128, 1024]) if debug else None
    u_scr = nc.dram_tensor("u_scr", [N_OC_IN, 128, T], F32, kind="Internal").ap()

    with ExitStack() as es:
        P = Prog(nc, es)
        U = Tile(u_scr, N_OC_IN, "u_scr")

        x = P.sb("x", [128, KC, T], F32, KC)
        hraw = P.sb("hraw", [128, KC * T // 2], F32, KC * 5)
        hb_ap = hraw.h[:].bitcast(BF16).rearrange("p (k t) -> p k t", t=T)
        mixed = P.sb("mixed", [128, KC, T], BF16, KC)
        ring = [P.sb("ring%d" % i, [128, 2048], BF16) for i in range(3)]
        ring_ch = [P.chan("ring%d" % i) for i in range(3)]
        fa = [P.sb("fa%d" % i, [128, TK if i < 3 else T], F32, 3) for i in range(4)]
        fa_ch = [P.chan("fa%d" % i) for i in range(4)]
        consts = P.sb("consts", [128, NCONST, 128], F32)
        id_b = P.sb("id_b", [128, 128], BF16)
        one_b = P.sb("one_b", [128, 128], BF16)
        pl = P.sb("pl", [128, NPC], F32)
        tab = P.sb("tab", [128, NTC], F32)
        sc_b = P.sb("sc_b", [128, KC, NS], BF16)
        mods = P.sb("mods", [128, 96, NS], F32)
        coef = P.sb("coef", [128, 2, KC, NS], F32)
        tokp = P.sb("tokp", [128, NT, 32], F32)
        col = [P.sb("col%d" % i, [128, 16], F32) for i in range(6)]
        epsc = P.sb("epsc", [128, 4], F32)
        pad2 = [P.sb("pad%d" % i, [128, 128], BF16) for i in range(2)]
        ptile = P.sb("ptile", [128, 2 * SL], BF16, 2)
        bank = [P.psum("bank%d" % i, [128, 512], F32) for i in range(7)]
        bankT = P.psum("bankT", [128, 1024], BF16)
        ch_consts, ch_tab, ch_rc, ch_rs, ch_ini = [P.chan(n) for n in ("consts", "tab", "rc", "rs", "ini")]
        ch_x = P.chan("x")
        ch_pl = P.chan("pl")
        ch_out = [P.chan("out%d" % i) for i in range(4)]
        ch_stg = P.chan("stg")
        ch_stg2 = [P.chan("stg0"), P.chan("stg1")]
        ch_ini2 = [P.chan("ini0"), P.chan("ini1")]
        ch_sts4 = [P.chan("sts%d" % i) for i in range(4)]
        ch_ini4 = [P.chan("ini4_%d" % i) for i in range(4)]
        ch_sts = P.chan("sts")
        state = {"bank": 0, "ring": 0, "uw": 0, "out": 0, "sm": 0, "smb": 0}
        reserved = set()

        class HView:
            pass
        def hv(kc, t0=0, t1=T, rows=None):
            e0, e1 = kc * T + t0, kc * T + t1
            cells = tuple(range(e0 // 256, (e1 - 1) // 256 + 1))
            ap = hb_ap[:, kc, t0:t1] if rows is None else hb_ap[rows[0]:rows[1], kc, t0:t1]
            return View(hraw, ap, cells)

        def hflat(kc0, nk, n, rows=None):
            e0 = kc0 * T
            cells = tuple(range(e0 // 256, (e0 + n - 1) // 256 + 1))
            ap = hb_ap[:, kc0:kc0 + nk, :].rearrange("p a t -> p (a t)")[:, 0:n]
            if rows is not None:
                ap = hb_ap[rows[0]:rows[1], kc0:kc0 + nk, :].rearrange("p a t -> p (a t)")[:, 0:n]
            return View(hraw, ap, cells)

        def smt(i):
            return View(hraw, hraw.h[:, i * 128:(i + 1) * 128], (i,))

        class _T:
            def __init__(self, i0, n=1):
                self.vw = View(hraw, hraw.h[:, i0 * 128:(i0 + n) * 128], tuple(range(i0, i0 + n)))
                self.h = hraw.h[:, i0 * 128:(i0 + n) * 128]
            def v(self):
                return self.vw
            def __getitem__(self, idx):
                return self.vw[idx]
        S_g, S_s, ini_t = _T(38), _T(38), _T(39)
        gtok_v = View(hraw, hraw.h[:, 70 * 128:70 * 128 + NT * 8].rearrange("p (t c) -> p t c", c=8), (70,))
        btok_v = View(hraw, hraw.h[:, 71 * 128:71 * 128 + NT * 8].rearrange("p (t c) -> p t c", c=8), (71,))
        dtok_v = View(hraw, hraw.h[:, 20 * 128:20 * 128 + NT * 16].rearrange("p (t c) -> p t c", c=16), (20, 21))
        atok_v = View(hraw, hraw.h[:, 22 * 128:22 * 128 + NT * 16].rearrange("p (t c) -> p t c", c=16), (22, 23))

        def smbt(i):
            e0 = 14 * T + i * 128
            return View(hraw, hb_ap[:, 14:16, :].rearrange("p a t -> p (a t)")[:, i * 128:(i + 1) * 128], (e0 // 256,))

        def nbank(pool=None):
            if pool is not None:
                pool[0] = (pool[0] + 1) % (len(pool) - 1)
                return bank[pool[1 + pool[0]]].v()
            while True:
                state["bank"] = (state["bank"] + 1) % 7
                if state["bank"] not in reserved:
                    return bank[state["bank"]].v()

        def nsm():
            state["sm"] = (state["sm"] + 1) % 6
            return smt(32 + state["sm"])

        def nsmb():
            state["smb"] = (state["smb"] + 1) % 12
            return smbt(state["smb"])

        def cst(i):
            return consts[:, i, :]

        ring_v = [(ring[i].v(), ring_ch[i]) for i in range(3)]
        extra_v = {k: (View(fa[k], fa[k].h[:].bitcast(BF16)[:, 0:2048], (0, 1, 2)), fa_ch[k]) for k in (0, 1, 2, 3)}
        state["extra"] = ()

        def wload(src_ap, ncols):
            slots = ring_v + [extra_v[k] for k in state["extra"]]
            i = state["ring"] % len(slots)
            state["ring"] += 1
            slot, ch = slots[i]
            P.dma("gpsimd", ch, slot.ap[:, 0:ncols], src_ap, out_view=slot)
            return slot

        def uload(fi, src_ap, cells, n=T):
            P.dma("sync", fa_ch[fi], fa[fi].h[:, 0:n], src_ap, out_view=fa[fi].v(), in_view=U.v(src_ap, cells))

        P.dma("sync", ch_consts, consts.h[:], d_const, out_view=consts.v())
        P.dma("sync", ch_tab, tab.h[:], d_tab, out_view=tab.v())
        P.dma("sync", fa_ch[0], fa[0].h[:, 0:KC * NS], d_cond.rearrange("p k s -> p (k s)"), out_view=fa[0].v())
        P.dma("sync", ch_x, x.h[:], d_x, out_view=x.v())
        P.copy(id_b.v(), cst(C_ID), "vector")
        P.copy(one_b.v(), cst(C_ONE), "vector")
        P.memset(epsc[:, 0:1], EPS)
        P.memset(epsc[:, 1:2], 1.0)
        P.memset(epsc[:, 2:3], 192.0 * EPS)
        P.act(sc_b.v(sc_b.h[:].rearrange("p k s -> p (k s)")), fa[0][:, 0:KC * NS], AF.Silu)
        for p_ in pad2:
            P.memset(p_.v(), 0.0)

        def sumsq_rstd(chunks, n, dim, out_f32, sq_fn):
            for (c0, cn) in [(a_, min(512, n - a_)) for a_ in range(0, n, 512)]:
                b_ = nbank()
                for i, cv in enumerate(chunks):
                    kp = cv.ap.shape[0]
                    sq = sq_fn(i, cn)
                    P.act(sq[0:kp, :], cv[:, c0:c0 + cn], AF.Square)
                    P.mm(b_[:, 0:cn], one_b[0:kp, :], sq[0:kp, :], start=(i == 0), stop=(i == len(chunks) - 1))
                P.act(out_f32[:, c0:c0 + cn], b_[:, 0:cn], AF.Sqrt, bias=epsc[:, 0:1], scale=1.0 / dim)
                P.recip(out_f32[:, c0:c0 + cn], out_f32[:, c0:c0 + cn])

        def sq_mixed(i, cn):
            return mixed.v(mixed.h[:, 14 + (i % 2), 0:cn], 14 + (i % 2))

        def sq_h(i, cn):
            return hv(13, (i % 2) * 512, (i % 2) * 512 + cn)

        def modnorm(which):
            rstd = fa[3]
            sumsq_rstd([x.v(x.h[:, kc, :], kc) for kc in range(KC)], T, D, rstd, sq_mixed)
            shift_oc = 0 if which == 0 else 48
            for kc in range(KC):
                tmp = fa[kc % 2]
                P.tt(tmp[:, 0:T], x.v(x.h[:, kc, :], kc), rstd[:, 0:T], ALU.mult)
                for s_ in range(NS):
                    P.act(hv(kc, s_ * SL, (s_ + 1) * SL), tmp[:, s_ * SL:(s_ + 1) * SL], AF.Identity,
                          bias=mods[:, shift_oc + kc, s_:s_ + 1], scale=coef[:, which, kc, s_:s_ + 1])

        def dense(src_fn, n_oc, kcn, rhs_fn, evac):
            for oc in range(n_oc):
                slot = wload(src_fn(oc), kcn * 128)
                for gi, (t0, tn) in enumerate(TG):
                    b_ = nbank()
                    for kc in range(kcn):
                        P.mm(b_[:, 0:tn], slot[:, kc * 128:(kc + 1) * 128], rhs_fn(kc, t0, t0 + tn),
                             start=(kc == 0), stop=(kc == kcn - 1))
                    evac(oc, gi, (t0, tn), b_)

        def evac_res(gate_oc):
            def f_(oc, gi, tg, b_):
                t0, tn = tg
                for s_ in range(NS):
                    a0, a1 = max(t0, s_ * SL), min(t0 + tn, (s_ + 1) * SL)
                    if a1 <= a0:
                        continue
                    P.stt(x.v(x.h[:, oc, a0:a1], oc), b_[:, a0 - t0:a1 - t0], mods[:, gate_oc + oc, s_:s_ + 1],
                          x.v(x.h[:, oc, a0:a1], oc), ALU.mult, ALU.add)
            return f_

        def rope(v64, tmp64):
            for (t0, tn) in TG:
                b_ = nbank()
                P.mm(b_[0:64, 0:tn], consts[0:64, C_ROT, 0:64], v64[:, t0:t0 + tn])
                P.tt(tmp64[:, 0:tn], b_[0:64, 0:tn], mixed.v(mixed.h[0:64, 15, t0:t0 + tn], 15), ALU.mult)
                P.tt(v64[:, t0:t0 + tn], v64[:, t0:t0 + tn], mixed.v(mixed.h[0:64, 14, t0:t0 + tn], 14), ALU.mult)
                P.tt(v64[:, t0:t0 + tn], v64[:, t0:t0 + tn], tmp64[:, 0:tn], ALU.add)

        def ada_mm(l_, ocs):
            mb_ = bank[0].v()
            for oc in ocs:
                slot = wload(d_ada[l_, oc].rearrange("p k c -> p (k c)"), KC * 128)
                for kc in range(KC):
                    P.mm(mb_[:, oc * NS:(oc + 1) * NS], slot[:, kc * 128:(kc + 1) * 128], sc_b[:, kc, :],
                         start=(kc == 0), stop=(kc == KC - 1))

        for l in range(n_layers):
            P.dma("sync", ch_pl, pl.h[:], d_pl[l], out_view=pl.v())
            mb = bank[0].v()
            if l == 0:
                state["extra"] = (0, 1, 2, 3)
                reserved.add(0)
                ada_mm(0, range(96))
            for s_ in range(NS):
                P.tt(mods[:, :, s_], View(mb.tile, mb.ap[:, 0:96 * NS].rearrange("p (o s) -> p o s", s=NS)[:, :, s_], mb.cells),
                     pl[:, PC_ADAB:PC_ADAB + 96], ALU.add)
            reserved.discard(0)
            for which, (scale_oc, pcg) in enumerate(((16, PC_N1), (64, PC_N2))):
                for s_ in range(NS):
                    P.stt(coef[:, which, :, s_], mods[:, scale_oc:scale_oc + KC, s_], 1.0, pl[:, pcg:pcg + KC], ALU.add, ALU.mult)

            state["extra"] = ()
            modnorm(0)
            state["extra"] = (2, 3)

            def evac_in(oc, gi, tg, b_):
                t0, tn = tg
                i = state["uw"] % 2
                state["uw"] += 1
                P.copy(fa[i][:, 0:tn], b_[:, 0:tn])
                P.dma("sync", fa_ch[i], u_scr[oc, :, t0:t0 + tn], fa[i].h[:, 0:tn],
                      out_view=U.v(u_scr[oc, :, t0:t0 + tn], oc), in_view=fa[i].v())
            dense(lambda oc: d_win[l, oc].rearrange("p k c -> p (k c)"), N_OC_IN, KC, hv, evac_in)

            state["extra"] = ()
            wsm = wload(d_wsm[l].rearrange("p k c -> p (k c)"), KC * 32)
            for tt_ in range(NT):
                b_ = nbank()
                for kc in range(KC):
                    P.mm(b_[:, 0:32], hv(kc, tt_ * 128, (tt_ + 1) * 128), wsm[:, kc * 32:(kc + 1) * 32], start=(kc == 0), stop=(kc == KC - 1))
                P.copy(tokp[:, tt_, :], b_[:, 0:32], "vector")
            P.dma("gpsimd", ch_rc, mixed.h[0:64, 14, :], d_ropec, out_view=mixed.v(mixed.h[0:64, 14, :], 14))
            P.dma("gpsimd", ch_rs, mixed.h[0:64, 15, :], d_ropes, out_view=mixed.v(mixed.h[0:64, 15, :], 15))
            rstd = fa[3]
            for c in range(6):
                uload(c % 2, u_scr[OC_CQ + c], OC_CQ + c)
                P.copy(hv(c), fa[c % 2][:, 0:T])
            sumsq_rstd([hv(c) for c in range(6)], T, 768.0, rstd, sq_h)
            for c in range(6):
                P.stt(mixed.v(mixed.h[:, 8 + c, :], 8 + c), hv(c), pl[:, PC_QN + c:PC_QN + c + 1], rstd[:, 0:T],
                      ALU.mult, ALU.mult)
            cqn = lambda kc, t0, t1: mixed.v(mixed.h[:, 8 + kc, t0:t1], 8 + kc)
            ckvn = lambda c, t0, t1: View(hraw, hb_ap[:, 8:13, :].rearrange("p a t -> p (a t)")[:, c * TK + t0:c * TK + t1],
                                          tuple(range((8 * T + c * TK + t0) // 256, (8 * T + c * TK + t1 - 1) // 256 + 1)))
            for c in range(4):
                uload(c % 2, u_scr[OC_CKV + c], OC_CKV + c)
                P.copy(hv(c), fa[c % 2][:, 0:T])
            sumsq_rstd([hv(c) for c in range(4)], T, 512.0, rstd, sq_h)
            for c in range(4):
                uload(c % 2, u_scr[OC_CKV + c], OC_CKV + c)
                P.stt(fa[c % 2][:, 0:T], fa[c % 2][:, 0:T], pl[:, PC_KVN + c:PC_KVN + c + 1], rstd[:, 0:T], ALU.mult, ALU.mult)
                P.copy(ckvn(c, 0, T), fa[c % 2][:, 0:T])
                P.dma("sync", fa_ch[c % 2], o_ckv[l, :, c, :], fa[c % 2].h[:, 0:T], in_view=fa[c % 2].v())
                P.dma("sync", fa_ch[2], fa[2].h[:, 0:256], d_ctxckv[l, :, c, :], out_view=fa[2].v())
                P.copy(ckvn(c, T, TK), fa[2][:, 0:256])
            kr = fa[2]
            P.dma("sync", fa_ch[2], kr.h[:, 0:T], u_scr[OC_KR], out_view=kr.v(), in_view=U.v(u_scr[OC_KR], OC_KR))
            P.dma("sync", ch_out[0], o_kr[l], kr.h[0:64, 0:T], in_view=kr.v())
            P.dma("sync", fa_ch[2], kr.h[0:64, T:TK], d_ctxkr[l], out_view=kr.v())
            krsq = hflat(14, 2, TK, rows=(0, 64))
            P.act(krsq, kr[0:64, 0:TK], AF.Square)
            sc = 192.0 ** -0.5

            for hd in range(8):
                vt = lambda a0, a1: View(hraw, hb_ap[:, 2:4, :].rearrange("p a t -> p (a t)")[:, a0:a1],
                                         tuple(range((2 * T + a0) // 256, (2 * T + a1 - 1) // 256 + 1)))
                knb = lambda a0, a1: View(hraw, hb_ap[:, 4:6, :].rearrange("p a t -> p (a t)")[:, a0:a1],
                                          tuple(range((4 * T + a0) // 256, (4 * T + a1 - 1) // 256 + 1)))
                krb = lambda a0, a1: View(hraw, hb_ap[0:64, 6:8, :].rearrange("p a t -> p (a t)")[:, a0:a1],
                                          tuple(range((6 * T + a0) // 256, (6 * T + a1 - 1) // 256 + 1)))

                def rope_ip(v64, t0, tn, bkp_):
                    b4 = nbank(bkp_)
                    P.mm(b4[0:64, 0:tn], consts[0:64, C_ROT, 0:64], v64)
                    P.tt(b4[0:64, 0:tn], b4[0:64, 0:tn], mixed.v(mixed.h[0:64, 15, t0:t0 + tn], 15), ALU.mult)
                    P.tt(v64, v64, mixed.v(mixed.h[0:64, 14, t0:t0 + tn], 14), ALU.mult)
                    return b4

                def q_chain(hd=hd):
                    bkp_ = [0, 0, 1, 2]
                    slot = wload(d_wuq[l, hd].rearrange("p k c -> p (k c)"), 6 * 192)
                    for (t0, tn) in TG:
                        b1 = nbank(bkp_)
                        for kc in range(6):
                            P.mm(b1[:, 0:tn], slot[:, kc * 192:kc * 192 + 128], cqn(kc, t0, t0 + tn), start=(kc == 0), stop=(kc == 5))
                        b2 = nbank(bkp_)
                        for kc in range(6):
                            P.mm(b2[0:64, 0:tn], slot[:, kc * 192 + 128:kc * 192 + 192], cqn(kc, t0, t0 + tn), start=(kc == 0), stop=(kc == 5))
                        yield
                        b3 = nbank(bkp_)
                        sq = ptile.v(ptile.h[:, 0:tn], (0, 1))
                        P.act(sq, b1[:, 0:tn], AF.Square)
                        P.mm(b3[:, 0:tn], one_b.v(), sq, start=True, stop=False)
                        yield
                        P.act(sq[0:64, :], b2[0:64, 0:tn], AF.Square)
                        P.mm(b3[:, 0:tn], one_b[0:64, :], sq[0:64, :], start=False, stop=True)
                        yield
                        rg = fa[3].v(fa[3].h[:, 0:tn], 0)
                        P.act(rg, b3[:, 0:tn], AF.Sqrt, bias=epsc[:, 2:3], scale=1.0)
                        yield
                        P.recip(rg, rg)
                        P.stt(hv(0, t0, t0 + tn), b1[:, 0:tn], pl[:, PC_QG:PC_QG + 1], rg, ALU.mult, ALU.mult)
                        qs = fa[3].v(fa[3].h[0:64, 512:512 + tn], 1)
                        P.stt(qs, b2[0:64, 0:tn], pl[0:64, PC_QG + 1:PC_QG + 2], rg[0:64, :], ALU.mult, ALU.mult)
                        b4 = rope_ip(qs, t0, tn, bkp_)
                        yield
                        P.tt(hv(1, t0, t0 + tn, rows=(0, 64)), qs, b4[0:64, 0:tn], ALU.add)
                        yield

                def k_chain(hd=hd):
                    bkp_ = [0, 3, 4, 5, 6]
                    slot = wload(d_wukv[l, hd].rearrange("p k c -> p (k c)"), 4 * 256)
                    kn = fa[0]
                    KG = [(0, 512), (512, 512), (1024, 512)]
                    for (t0, tn) in KG:
                        b_ = nbank(bkp_)
                        for kc in range(4):
                            P.mm(b_[:, 0:tn], slot[:, kc * 256:kc * 256 + 128], ckvn(kc, t0, t0 + tn), start=(kc == 0), stop=(kc == 3))
                        yield
                        P.copy(kn[:, t0:t0 + tn], b_[:, 0:tn])
                    for kt in range(TK // 128):
                        if kt % 4 == 0:
                            b_ = nbank(bkp_)
                        for kc in range(4):
                            P.mm(b_[:, (kt % 4) * 128:(kt % 4 + 1) * 128], ckvn(kc, kt * 128, (kt + 1) * 128),
                                 slot[:, kc * 256 + 128:kc * 256 + 256], start=(kc == 0), stop=(kc == 3))
                        if kt % 4 == 3:
                            yield
                            P.copy(vt((kt - 3) * 128, (kt + 1) * 128), b_[:, 0:512])
                    krstd = fa[1]
                    for (t0, tn) in KG:
                        b_ = nbank(bkp_)
                        sq = sq_h(0, tn)
                        P.act(sq, kn[:, t0:t0 + tn], AF.Square)
                        yield
                        P.mm(b_[:, 0:tn], one_b.v(), sq, start=True, stop=False)
                        P.mm(b_[:, 0:tn], one_b[0:64, :], krsq[:, t0:t0 + tn], start=False, stop=True)
                        yield
                        P.act(krstd[:, t0:t0 + tn], b_[:, 0:tn], AF.Sqrt, bias=epsc[:, 0:1], scale=1.0 / 192)
                        yield
                        P.recip(krstd[:, t0:t0 + tn], krstd[:, t0:t0 + tn])
                    P.stt(knb(0, TK), kn[:, 0:TK], pl[:, PC_KG:PC_KG + 1], krstd[:, 0:TK], ALU.mult, ALU.mult)
                    yield
                    krh = fa[0]
                    P.stt(krh[0:64, 0:TK], kr[0:64, 0:TK], pl[0:64, PC_KG + 1:PC_KG + 2], krstd[0:64, 0:TK], ALU.mult, ALU.mult)
                    yield
                    for (t0, tn) in TG:
                        b4 = rope_ip(krh[0:64, t0:t0 + tn], t0, tn, bkp_)
                        yield
                        P.tt(krh[0:64, t0:t0 + tn], krh[0:64, t0:t0 + tn], b4[0:64, 0:tn], ALU.add)
                    yield
                    P.copy(krb(0, TK), krh[0:64, 0:TK], "vector")

                gens = [q_chain(), k_chain()]
                while gens:
                    for g_ in list(gens):
                        try:
                            next(g_)
                        except StopIteration:
                            gens.remove(g_)
                reserved.update((0, 1))
                for s_ in range(NS):
                    blocks = [0] if s_ == 0 else [1, 2, 3, 4, 5]
                    q0 = s_ * SL
                    ob, db = bank[0].v(), bank[1].v()
                    nk = len(blocks) * 2
                    for bi, blk in enumerate(blocks):
                        for half in range(2):
                            i_ = bi * 2 + half
                            k0 = blk * SL + half * 128
                            sb_ = nbank()
                            P.mm(sb_[:, 0:SL], knb(k0, k0 + 128), hv(0, q0, q0 + SL), start=True, stop=False)
                            P.mm(sb_[:, 0:SL], krb(k0, k0 + 128), hv(1, q0, q0 + SL, rows=(0, 64)), start=False, stop=True)
                            pt = ptile.v(ptile.h[:, (i_ % 2) * SL:(i_ % 2 + 1) * SL], i_ % 2)
                            mc = TC_MASK + s_ * 6 + blk
                            P.act(pt, sb_[:, 0:SL], AF.Exp, bias=tab[:, mc:mc + 1])
                            P.mm(ob[:, 0:SL], vt(k0, k0 + 128), pt, start=(i_ == 0), stop=(i_ == nk - 1))
                            P.mm(db[:, 0:SL], one_b.v(), pt, start=(i_ == 0), stop=(i_ == nk - 1))
                    rc = fa[1]
                    P.recip(rc[:, 0:SL], db[:, 0:SL])
                    P.tt(mixed.v(mixed.h[:, hd, q0:q0 + SL], hd), ob[:, 0:SL], rc[:, 0:SL], ALU.mult)
                reserved.difference_update((0, 1))

            tk = PC_TOK
            P.act(col[0][:, 0:8], pl[:, tk:tk + 8], AF.Exp)
            for tt_ in range(NT):
                P.act(btok_v[:, tt_, :], tokp[:, tt_, 0:8], AF.Sigmoid)
                P.tt(gtok_v[:, tt_, :], tokp[:, tt_, 8:16], pl[:, tk + 8:tk + 16], ALU.add)
                P.act(gtok_v[:, tt_, :], gtok_v[:, tt_, :], AF.Exp)
                P.act(gtok_v[:, tt_, :], gtok_v[:, tt_, :], AF.Ln, bias=epsc[:, 1:2])
                P.stt(gtok_v[:, tt_, :], gtok_v[:, tt_, :], -1.0, col[0][:, 0:8], ALU.mult, ALU.mult)

            def conv_silu(oc, wcol, bias_col, fo):
                xp = fa[2]
                xpv = View(xp, xp.h[:, 0:NS * 260].rearrange("p (s t) -> p s t", t=260), (0,))
                P.dma("sync", fa_ch[2], xpv.ap[:, :, 2:258], u_scr[oc].rearrange("p (s t) -> p s t", t=SL),
                      out_view=xp.v(), in_view=U.v(u_scr[oc], oc))
                P.memset(xpv[:, 0, 0:2], 0.0)
                P.memset(xpv[:, NS - 1, 258:260], 0.0)
                al = View(tab, tab.h[:, TC_AL:TC_AL + 8].rearrange("p (s t) -> p s t", t=2), (0,))
                ar = View(tab, tab.h[:, TC_AR:TC_AR + 8].rearrange("p (s t) -> p s t", t=2), (0,))
                P.tt(xpv[:, 1:NS, 0:2], xpv[:, 0:NS - 1, 256:258], al, ALU.mult)
                P.tt(xpv[:, 0:NS - 1, 258:260], xpv[:, 1:NS, 2:4], ar, ALU.mult)
                o3 = View(fa[fo], fa[fo].h[:, 0:T].rearrange("p (s t) -> p s t", t=SL), (0,))
                P.ts(o3, xpv[:, :, 0:256], pl[:, wcol:wcol + 1], None, ALU.mult)
                for j in range(1, 5):
                    P.stt(o3, xpv[:, :, j:j + 256], pl[:, wcol + j:wcol + j + 1], o3, ALU.mult, ALU.add)
                if bias_col is None:
                    P.act(fa[fo][:, 0:T], fa[fo][:, 0:T], AF.Silu)
                else:
                    P.act(fa[fo][:, 0:T], fa[fo][:, 0:T], AF.Silu, bias=pl[:, bias_col:bias_col + 1])

            def silu_load(oc, fo):
                uload(fo, u_scr[oc], oc)
                P.act(fa[fo][:, 0:T], fa[fo][:, 0:T], AF.Silu)

            def transpose_tiles(src_fn, dst_fn):
                for tt_ in range(NT):
                    bt = bankT.v()
                    P.transpose(bt[:, 0:128], src_fn(tt_), id_b.v())
                    P.copy(dst_fn(tt_), bt[:, 0:128])

            QT, KT, VT, KTOK, VTOK = 8, 9, 10, 11, 12
            for hd in range(4):
                rs = fa[3]
                conv_silu(OC_GQ + hd, PC_GCW + hd * 5, None, 0)
                sumsq_rstd([fa[0][:, 0:T]], T, 1.0, rs, sq_h)
                P.stt(hv(QT), fa[0][:, 0:T], 128.0 ** -0.5, rs[:, 0:T], ALU.mult, ALU.mult)
                conv_silu(OC_GK + hd, PC_GCW + (4 + hd) * 5, None, 0)
                sumsq_rstd([fa[0][:, 0:T]], T, 1.0, rs, sq_h)
                P.tt(hv(KT), fa[0][:, 0:T], rs[:, 0:T], ALU.mult)
                conv_silu(OC_GV + hd, PC_GCW + (8 + hd) * 5, None, 0)
                P.copy(hv(VT), fa[0][:, 0:T])
                oacc = fa[1]
                transpose_tiles(lambda t_: hv(KT, t_ * 128, (t_ + 1) * 128), lambda t_: hv(KTOK, t_ * 128, (t_ + 1) * 128))
                transpose_tiles(lambda t_: hv(VT, t_ * 128, (t_ + 1) * 128), lambda t_: hv(VTOK, t_ * 128, (t_ + 1) * 128))
                oacc_b = fa[2]

                def gdn_chain(dr, hd=hd):
                    tri, stri = (C_T64F, C_S64F) if dr == 0 else (C_T64B, C_S64B)
                    ci = dr * 4 + hd
                    base = dr * 17
                    g_bc, b_bc, Dm, egr, brow, egl, Nm, NT_, qkT, qg, vnew = [smt(base + i) for i in range(11)]
                    kbg, vb_, kdec, wT, uu = g_bc, b_bc, brow, Nm, NT_
                    pool = [smt(base + 11 + i) for i in range(6)]
                    pstate = [0]
                    bkp = [0, 0, 1, 2] if dr == 0 else [0, 3, 4, 5]

                    def npool():
                        pstate[0] = (pstate[0] + 1) % 6
                        return pool[pstate[0]]
                    Sg = _T(34 + dr)
                    oa = fa[1] if dr == 0 else oacc_b
                    cc = col[1 + dr]
                    P.memset(Sg.v(), 0.0)
                    order = list(range(NT)) if dr == 0 else list(range(NT - 1, -1, -1))
                    for tt_ in order:
                        s_ = tt_ // 2
                        first = (tt_ % 2 == 0) if dr == 0 else (tt_ % 2 == 1)
                        if first:
                            ac = TC_AS + dr * 5 + s_
                            P.ts(Sg.v(), Sg.v(), tab[:, ac:ac + 1], None, ALU.mult)
                            if (dr == 0 and s_ == 1) or (dr == 1 and s_ == 4):
                                it = _T(36 + dr)
                                P.dma("sync", ch_ini2[dr], it.h[:], d_ginit[l, dr, hd], out_view=it.v())
                                P.tt(Sg.v(), Sg.v(), it.v(), ALU.add)
                        tq = lambda c_: hv(c_, tt_ * 128, (tt_ + 1) * 128)
                        gcol = gtok_v[:, tt_, ci:ci + 1]
                        bcol = btok_v[:, tt_, ci:ci + 1]
                        P.ascale(g_bc, cst(C_ONE), gcol)
                        P.ascale(b_bc, cst(C_ONE), bcol)
                        bA = nbank(bkp)
                        P.mm(bA[:, 0:128], g_bc, cst(tri))
                        P.mm(bA[:, 128:256], b_bc, cst(C_ID))
                        P.mm(bA[:, 256:257], cst(tri), gcol)
                        P.mm(bA[:, 257:258], cst(C_BLK), gcol)
                        yield
                        P.copy(cc[:, 0:2], bA[:, 256:258], "scalar")
                        yield
                        P.act(cc[:, 2:3], cc[:, 0:1], AF.Exp)
                        yield
                        P.tt(cc[:, 3:4], cc[:, 2:3], bcol, ALU.mult)
                        P.tt(cc[:, 4:5], cc[:, 1:2], cc[:, 0:1], ALU.subtract)
                        yield
                        P.act(cc[:, 4:5], cc[:, 4:5], AF.Exp)
                        P.ts(Dm, bA[:, 0:128], cc[:, 0:1], 0.0, ALU.subtract, ALU.min)
                        yield
                        P.act(Dm, Dm, AF.Exp)
                        yield
                        P.tt(Dm, Dm, cst(tri), ALU.mult)
                        P.act(egr, bA[:, 0:128], AF.Exp)
                        yield
                        P.copy(brow, bA[:, 128:256], "scalar")
                        lastc = (63, 127) if dr == 0 else (0, 64)
                        P.copy(egl[:, 0:1], egr[:, lastc[0]:lastc[0] + 1], "vector")
                        P.copy(egl[:, 1:2], egr[:, lastc[1]:lastc[1] + 1], "vector")
                        bB = nbank(bkp)
                        P.mm(bB[:, 0:128], tq(KT), tq(KT))
                        P.mm(bB[:, 128:256], tq(KT), tq(QT))
                        yield
                        P.stt(Nm, bB[:, 0:128], -1.0, Dm, ALU.mult, ALU.mult)
                        P.tt(Nm, Nm, cst(stri), ALU.mult)
                        P.tt(Nm, Nm, brow, ALU.mult)
                        P.tt(qkT, bB[:, 128:256], Dm, ALU.mult)
                        yield
                        bC = nbank(bkp)
                        P.transpose(bC[:, 0:128], Nm, cst(C_ID))
                        yield
                        P.copy(NT_, bC[:, 0:128], "scalar")
                        pairs = [View(hraw, hraw.h[:, (base + 11 + 2 * i) * 128:(base + 13 + 2 * i) * 128],
                                      (base + 11 + 2 * i, base + 12 + 2 * i)) for i in range(2)]
                        singles = [smt(base + 15), smt(base + 16)]
                        bD = nbank(bkp)
                        P.mm(bD[:, 128:256], NT_, Nm)
                        P.mm(bD[:, 256:384], Nm, NT_)
                        yield
                        PA, AT = pairs[0], singles[0]
                        P.tt(PA[:, 0:128], Nm, cst(C_ID), ALU.add)
                        P.copy(PA[:, 128:256], bD[:, 128:256], "scalar")
                        P.copy(AT, bD[:, 256:384], "vector")
                        yield
                        for lev in range(1, 6):
                            bD = nbank(bkp)
                            if lev < 5:
                                P.mm(bD[:, 0:256], AT, PA[:, 0:256])
                                P.mm(bD[:, 256:384], PA[:, 128:256], AT)
                            else:
                                P.mm(bD[:, 0:128], AT, PA[:, 0:128])
                            yield
                            PA2, AT2 = pairs[lev % 2], singles[lev % 2]
                            P.tt(PA2[:, 0:128], PA[:, 0:128], bD[:, 0:128], ALU.add)
                            if lev < 5:
                                P.copy(PA2[:, 128:256], bD[:, 128:256], "scalar")
                                P.copy(AT2, bD[:, 256:384], "scalar")
                            yield
                            PA, AT = PA2, AT2
                        Pm = PA[:, 0:128]
                        TmT = Pm
                        P.ascale(kbg, tq(KTOK), cc[:, 3:4])
                        P.ascale(vb_, tq(VTOK), bcol)
                        P.ascale(kdec, tq(KTOK), cc[:, 4:5])
                        yield
                        bF = nbank(bkp)
                        P.mm(bF[:, 0:128], kbg, TmT)
                        P.mm(bF[:, 128:256], TmT, vb_)
                        yield
                        P.copy(wT, bF[:, 0:128], "scalar")
                        P.copy(uu, bF[:, 128:256], "scalar")
                        P.tt(qg, tq(QT), egr, ALU.mult)
                        yield
                        for c in ((0, 1) if dr == 0 else (1, 0)):
                            r = slice(c * 64, c * 64 + 64)
                            bG = nbank(bkp)
                            P.mm(bG[:, 0:128], wT, Sg.v())
                            yield
                            P.tt(vnew[r, :], uu[r, :], bG[r, 0:128], ALU.subtract)
                            P.mm(bG[:, 128:192], Sg.v(), qg[:, r], start=True, stop=False)
                            P.mm(bG[:, 128:192], vnew[r, :], qkT[r, r], start=False, stop=True)
                            P.mm(bG[:, 256:384], kdec[r, :], vnew[r, :])
                            yield
                            t0 = tt_ * 128 + c * 64
                            P.copy(oa[:, t0:t0 + 64], bG[:, 128:192], "scalar")
                            P.stt(Sg.v(), Sg.v(), egl[:, c:c + 1], bG[:, 256:384], ALU.mult, ALU.add)
                        last = (tt_ % 2 == 1) if dr == 0 else (tt_ % 2 == 0)
                        if last:
                            P.dma("sync", ch_stg2[dr], o_gdn[l, s_, dr, hd], Sg.h, in_view=Sg.v())

                gens = [gdn_chain(0), gdn_chain(1)]
                while gens:
                    for g_ in list(gens):
                        try:
                            next(g_)
                        except StopIteration:
                            gens.remove(g_)
                P.tt(oacc[:, 0:T], oacc[:, 0:T], oacc_b[:, 0:T], ALU.add)
                sumsq_rstd([oacc[:, 0:T]], T, 128.0, rs, sq_h)
                P.stt(oacc[:, 0:T], oacc[:, 0:T], pl[:, PC_GNG:PC_GNG + 1], rs[:, 0:T], ALU.mult, ALU.mult)
                silu_load(OC_GZ + hd, 0)
                P.tt(mixed.v(mixed.h[:, 8 + hd, :], 8 + hd), oacc[:, 0:T], fa[0][:, 0:T], ALU.mult)

            BTc, CTc, BTOK, XB, XTOK = 8, 9, 10, 11, 12
            P.act(col[5][:, 0:16], pl[:, PC_TOK + 16:PC_TOK + 32], AF.Exp)
            for tt_ in range(NT):
                P.tt(dtok_v[:, tt_, :], tokp[:, tt_, 16:32], pl[:, PC_TOK + 32:PC_TOK + 48], ALU.add)
                P.act(dtok_v[:, tt_, :], dtok_v[:, tt_, :], AF.Exp)
                P.act(dtok_v[:, tt_, :], dtok_v[:, tt_, :], AF.Ln, bias=epsc[:, 1:2])
                P.stt(atok_v[:, tt_, :], dtok_v[:, tt_, :], -1.0, col[5][:, 0:16], ALU.mult, ALU.mult)
            for grp in range(2):
                conv_silu(OC_SB + grp, PC_SCW + (4 + grp) * 5, PC_SCB + 4 + grp, 0)
                P.copy(hv(BTc), fa[0][:, 0:T])
                conv_silu(OC_SC + grp, PC_SCW + (6 + grp) * 5, PC_SCB + 6 + grp, 0)
                P.copy(hv(CTc), fa[0][:, 0:T])
                transpose_tiles(lambda t_: hv(BTc, t_ * 128, (t_ + 1) * 128), lambda t_: hv(BTOK, t_ * 128, (t_ + 1) * 128))
                for pair in range(2):
                    chn = grp * 2 + pair
                    conv_silu(OC_SX + chn, PC_SCW + chn * 5, PC_SCB + chn, 0)
                    P.copy(hv(XB), fa[0][:, 0:T])
                    transpose_tiles(lambda t_: hv(XB, t_ * 128, (t_ + 1) * 128), lambda t_: hv(XTOK, t_ * 128, (t_ + 1) * 128))
                    yacc = fa[1]
                    P.memset(yacc[:, 0:T], 0.0)
                    def ssd_chain(par, dr, chn=chn):
                        tri = C_T128F if dr == 0 else C_T128B
                        cidx = dr * 2 + par
                        bkp = [[0, 0, 1], [0, 2, 3], [0, 4, 5], [0, 6]][cidx]
                        hd = chn * 2 + par
                        ci = dr * 8 + hd
                        hc = slice(par * 64, par * 64 + 64)
                        a_bc, Lm, ear = smt(cidx * 3), smt(cidx * 3 + 1), smt(cidx * 3 + 2)
                        Ss = _T(12 + cidx)
                        scT, ceT, xdte, Sb = [smbt(cidx * 4 + i) for i in range(4)]
                        pd = smbt(16 + cidx)
                        cc = col[cidx] if cidx < 3 else col[4]
                        P.memset(pd, 0.0)
                        P.memset(Ss.v(), 0.0)
                        order = list(range(NT)) if dr == 0 else list(range(NT - 1, -1, -1))
                        for tt_ in order:
                            s_ = tt_ // 2
                            first = (tt_ % 2 == 0) if dr == 0 else (tt_ % 2 == 1)
                            if first:
                                ac = TC_AS + dr * 5 + s_
                                P.ts(Ss[:, hc], Ss[:, hc], tab[:, ac:ac + 1], None, ALU.mult)
                                if (dr == 0 and s_ == 1) or (dr == 1 and s_ == 4):
                                    it = _T(16 + cidx)
                                    P.dma("sync", ch_ini4[cidx], it.h[:, 0:64], d_sinit[l, dr, hd], out_view=it.v())
                                    P.tt(Ss[:, hc], Ss[:, hc], it[:, 0:64], ALU.add)
                            tq = lambda c_: hv(c_, tt_ * 128, (tt_ + 1) * 128)
                            acol = atok_v[:, tt_, ci:ci + 1]
                            dcol = dtok_v[:, tt_, ci:ci + 1]
                            P.ascale(a_bc, cst(C_ONE), acol)
                            bA = nbank(bkp)
                            P.mm(bA[:, 0:128], a_bc, cst(tri))
                            P.mm(bA[:, 128:129], cst(tri), acol)
                            P.mm(bA[:, 129:130], cst(C_ONE), acol)
                            P.mm(bA[:, 256:384], tq(BTc), tq(CTc))
                            yield
                            P.copy(cc[:, 0:2], bA[:, 128:130], "scalar")
                            yield
                            P.tt(cc[:, 2:3], cc[:, 1:2], cc[:, 0:1], ALU.subtract)
                            yield
                            P.act(cc[:, 2:3], cc[:, 2:3], AF.Exp)
                            yield
                            P.tt(cc[:, 3:4], cc[:, 2:3], dcol, ALU.mult)
                            P.ts(Lm, bA[:, 0:128], cc[:, 0:1], 0.0, ALU.subtract, ALU.min)
                            yield
                            P.act(Lm, Lm, AF.Exp)
                            P.tt(Lm, Lm, cst(tri), ALU.mult)
                            P.act(ear, bA[:, 0:128], AF.Exp)
                            yield
                            P.tt(scT, bA[:, 256:384], Lm, ALU.mult)
                            P.tt(ceT, tq(CTc), ear, ALU.mult)
                            yield
                            lastc = 127 if dr == 0 else 0
                            P.ascale(pd[:, hc], tq(XTOK)[:, hc], dcol)
                            P.ascale(xdte[:, 0:64], tq(XTOK)[:, hc], cc[:, 3:4])
                            P.memset(Sb, 0.0)
                            P.copy(Sb[:, hc], Ss[:, hc], "scalar")
                            yield
                            bY = nbank(bkp)
                            P.mm(bY[:, 0:128], pd, scT, start=True, stop=False)
                            P.mm(bY[:, 0:128], Sb, ceT, start=False, stop=True)
                            P.mm(bY[:, 128:192], tq(BTOK), xdte[:, 0:64])
                            yield
                            tsl = slice(tt_ * 128, (tt_ + 1) * 128)
                            P.tt(yacc[:, tsl], yacc[:, tsl], bY[:, 0:128], ALU.add)
                            P.stt(Ss[:, hc], Ss[:, hc], ear[:, lastc:lastc + 1], bY[:, 128:192], ALU.mult, ALU.add)
                            last = (tt_ % 2 == 1) if dr == 0 else (tt_ % 2 == 0)
                            if last:
                                P.dma("sync", ch_sts4[cidx], o_ssd[l, s_, dr, hd], Ss.h[:, hc], in_view=Ss.v())

                    gens = [ssd_chain(0, 0), ssd_chain(1, 0), ssd_chain(0, 1), ssd_chain(1, 1)]
                    while gens:
                        for g_ in list(gens):
                            try:
                                next(g_)
                            except StopIteration:
                                gens.remove(g_)
                    P.stt(yacc[:, 0:T], fa[0][:, 0:T], pl[:, PC_SD + chn:PC_SD + chn + 1], yacc[:, 0:T], ALU.mult, ALU.add)
                    silu_load(OC_SZ + chn, 0)
                    dst = fa[3] if pair == 0 else fa[1]
                    P.tt(dst[:, 0:T], yacc[:, 0:T], fa[0][:, 0:T], ALU.mult)
                rs = fa[2]
                sumsq_rstd([fa[3][:, 0:T], fa[1][:, 0:T]], T, 256.0, rs, sq_h)
                for pair, src in enumerate((fa[3], fa[1])):
                    chn = grp * 2 + pair
                    P.stt(mixed.v(mixed.h[:, 12 + chn, :], 12 + chn), src[:, 0:T], pl[:, PC_SNG + chn:PC_SNG + chn + 1],
                          rs[:, 0:T], ALU.mult, ALU.mult)

            if debug and l == 0:
                P.dma("sync", ch_out[2], o_dbg, mixed.h[:], in_view=mixed.v())
            state["extra"] = (0, 1, 2, 3)
            dense(lambda oc: d_wout[l, oc].rearrange("p k c -> p (k c)"), KC, KC,
                  lambda kc, t0, t1: mixed.v(mixed.h[:, kc, t0:t1], kc), evac_res(32))

            state["extra"] = ()
            modnorm(1)
            state["extra"] = (1, 2, 3)
            if l + 1 < n_layers:
                reserved.add(0)
            FB = 8
            fblocks = [(b0, min(FB, FKC - b0)) for b0 in range(0, FKC, FB)]
            for bi, (b0, bn) in enumerate(fblocks):
                base = (bi % 2) * 8
                for j in range(bn):
                    f_ = b0 + j
                    sg = wload(d_wgu[l, f_].rearrange("p k c -> p (k c)"), KC * 128)
                    su = wload(d_wgu[l, 44 + f_].rearrange("p k c -> p (k c)"), KC * 128)
                    for gi_, (t0, tn) in enumerate(TG):
                        bg, bu = nbank(), nbank()
                        for kc in range(KC):
                            P.mm(bg[:, 0:tn], sg[:, kc * 128:(kc + 1) * 128], hv(kc, t0, t0 + tn), start=(kc == 0), stop=(kc == KC - 1))
                        for kc in range(KC):
                            P.mm(bu[:, 0:tn], su[:, kc * 128:(kc + 1) * 128], hv(kc, t0, t0 + tn), start=(kc == 0), stop=(kc == KC - 1))
                        sv = fa[0].v(fa[0].h[:, (gi_ % 2) * 512:(gi_ % 2) * 512 + tn], gi_ % 2)
                        P.act(sv, bg[:, 0:tn], AF.Silu)
                        P.tt(mixed.v(mixed.h[:, base + j, t0:t0 + tn], base + j), sv, bu[:, 0:tn], ALU.mult)
                    if l + 1 < n_layers:
                        ada_mm(l + 1, range(f_ * 96 // FKC, (f_ + 1) * 96 // FKC))
                for oc in range(KC):
                    sd = wload(d_wdn[l, oc, :, b0:b0 + bn, :].rearrange("p k c -> p (k c)"), bn * 128)
                    for gi, (t0, tn) in enumerate(TG):
                        b_ = nbank()
                        for j in range(bn):
                            P.mm(b_[:, 0:tn], sd[:, j * 128:(j + 1) * 128], mixed.v(mixed.h[:, base + j, t0:t0 + tn], base + j),
                                 start=(j == 0), stop=(j == bn - 1))
                        evac_res(80)(oc, gi, (t0, tn), b_)
            state["extra"] = ()

        P.dma("sync", ch_out[1], o_y, x.h[:], in_view=x.v())
        P.finish()
    return nc


def _tile_w(w, kc=None):
    K, N = w.shape
    return np.ascontiguousarray(w.reshape(K // 128, 128, N // 128, 128).transpose(2, 1, 0, 3))


def _consts():
    c = np.zeros((NCONST, 128, 128), np.float32)
    j = np.arange(128)[:, None]
    i = np.arange(128)[None, :]
    same = (j // 64) == (i // 64)
    c[C_ID] = np.eye(128)
    c[C_ONE] = 1.0
    c[C_T128F] = (i >= j)
    c[C_T128B] = (i <= j)
    c[C_T64F] = same & (i >= j)
    c[C_T64B] = same & (i <= j)
    c[C_S64F] = same & (i > j)
    c[C_S64B] = same & (i < j)
    c[C_BLK] = same
    for f in range(16):
        for base in (0, 32):
            c[C_ROT, base + 16 + f, base + f] = -1.0
            c[C_ROT, base + f, base + 16 + f] = 1.0
    return np.ascontiguousarray(c.transpose(1, 0, 2))


def _fm(v):
    return v.reshape(-1, 128).T


_CACHE = {}


def kernel(x_prompt, x_sample, cache_mla_ckv, cache_mla_krope, state_gdn, state_ssd, c, c_ctx,
           norm1_g, norm2_g, ada_w, ada_b, w_in, w_out, mla_qnorm_g, mla_w_uq, mla_kvnorm_g,
           mla_w_ukv, mla_q_g, mla_k_g, gdn_conv_w, gdn_a_log, gdn_dt_bias, gdn_norm_g,
           ssd_conv_w, ssd_conv_b, ssd_a_log, ssd_dt_bias, ssd_d, ssd_norm_g, ffn_w_gu, ffn_w_down):
    f = lambda a: np.asarray(a, np.float32)
    x_prompt, x_sample = f(x_prompt), f(x_sample)
    adaT = np.stack([_tile_w(f(ada_w[l])) for l in range(L)])
    w_in = f(w_in)
    secs = [(0, 768), (768, 1280), ("kr", 1280), (1344, 1856), (1856, 2368), (2368, 2880), (2880, 3392),
            (3408, 3920), (3920, 4432), (4432, 4688), (4688, 4944)]
    cols = []
    for a, b in secs:
        if a == "kr":
            cols.append(np.concatenate([np.arange(1280, 1344), np.full(64, -1)]))
        else:
            cols.append(np.arange(a, b))
    cols = np.concatenate(cols)
    win_r = np.where(cols[None, None, :] >= 0, w_in[:, :, np.maximum(cols, 0)], 0.0).astype(np.float32)
    winT = np.stack([_tile_w(win_r[l]) for l in range(L)])
    small_cols = np.concatenate([np.arange(3392, 3408), np.arange(4944, 4960)])
    wsm = np.ascontiguousarray(w_in[:, :, small_cols].reshape(L, KC, 128, 32).transpose(0, 2, 1, 3))
    wuq = np.ascontiguousarray(f(mla_w_uq).reshape(L, 6, 128, 8, 192).transpose(0, 3, 2, 1, 4))
    wukv = np.ascontiguousarray(f(mla_w_ukv).reshape(L, 4, 128, 8, 256).transpose(0, 3, 2, 1, 4))
    woutT = np.stack([_tile_w(f(w_out[l])) for l in range(L)])
    wguT = np.stack([_tile_w(f(ffn_w_gu[l])) for l in range(L)])
    wdnT = np.stack([_tile_w(f(ffn_w_down[l])) for l in range(L)])
    pl = np.zeros((L, 128, NPC), np.float32)
    for l in range(L):
        pl[l, :, PC_N1:PC_N1 + 16] = _fm(f(norm1_g[l]))
        pl[l, :, PC_N2:PC_N2 + 16] = _fm(f(norm2_g[l]))
        pl[l, :, PC_ADAB:PC_ADAB + 96] = _fm(f(ada_b[l]))
        pl[l, :, PC_QN:PC_QN + 6] = _fm(f(mla_qnorm_g[l]))
        pl[l, :, PC_KVN:PC_KVN + 4] = _fm(f(mla_kvnorm_g[l]))
        pl[l, :, PC_QG] = f(mla_q_g[l])[:128]
        pl[l, :64, PC_QG + 1] = f(mla_q_g[l])[128:]
        pl[l, :, PC_KG] = f(mla_k_g[l])[:128]
        pl[l, :64, PC_KG + 1] = f(mla_k_g[l])[128:]
        gcw = f(gdn_conv_w[l])
        pl[l, :, PC_GCW:PC_GCW + 60] = gcw.reshape(5, 12, 128).transpose(2, 1, 0).reshape(128, 60)
        scw = f(ssd_conv_w[l])
        pl[l, :, PC_SCW:PC_SCW + 40] = scw.reshape(5, 8, 128).transpose(2, 1, 0).reshape(128, 40)
        pl[l, :, PC_SCB:PC_SCB + 8] = _fm(f(ssd_conv_b[l]))
        pl[l, :, PC_GNG] = f(gdn_norm_g[l])
        pl[l, :, PC_SNG:PC_SNG + 4] = _fm(f(ssd_norm_g[l]))
        pl[l, :, PC_SD:PC_SD + 4] = _fm(np.repeat(f(ssd_d[l]), 64))
        pl[l, :, PC_TOK:PC_TOK + 8] = f(gdn_a_log[l]).reshape(-1)[None, :]
        pl[l, :, PC_TOK + 8:PC_TOK + 16] = f(gdn_dt_bias[l]).reshape(-1)[None, :]
        pl[l, :, PC_TOK + 16:PC_TOK + 32] = f(ssd_a_log[l]).reshape(-1)[None, :]
        pl[l, :, PC_TOK + 32:PC_TOK + 48] = f(ssd_dt_bias[l]).reshape(-1)[None, :]
    consts = _consts()
    rows = np.repeat(np.arange(16, dtype=np.float32), 64)
    colp = np.tile(np.arange(64, dtype=np.float32), 16)
    inv = (10000.0 ** (-np.arange(16, dtype=np.float32) / 16)).astype(np.float32)
    ang = np.stack([rows[:, None] * inv, colp[:, None] * inv], 1)
    cosf = np.cos(ang).astype(np.float32)
    sinf = np.sin(ang).astype(np.float32)
    cos64 = np.concatenate([cosf[:, 0], cosf[:, 0], cosf[:, 1], cosf[:, 1]], 1).T
    sin64 = np.concatenate([sinf[:, 0], sinf[:, 0], sinf[:, 1], sinf[:, 1]], 1).T

    shared = dict(consts=consts, pl=pl, ada_w=adaT, w_in=winT, w_small=wsm, w_uq=wuq, w_ukv=wukv,
                  w_out=woutT, w_gu=wguT, w_dn=wdnT)
    in_maps = []
    for core in range(8):
        if core < 6:
            xs = x_prompt[5 * core:5 * core + 5].reshape(T, D)
            cond = np.tile(f(c_ctx)[None, :], (NS, 1))
            samp = None
        else:
            b = core - 6
            xs = np.concatenate([x_prompt[30 + b], x_sample[b]], 0)
            cond = np.concatenate([f(c_ctx)[None, :], np.tile(f(c)[b][None, :], (4, 1))], 0)
            samp = b
        xT = np.ascontiguousarray(xs.T.reshape(KC, 128, T).transpose(1, 0, 2))
        condT = np.ascontiguousarray(cond.T.reshape(KC, 128, NS).transpose(1, 0, 2))
        tab = np.zeros((128, NTC), np.float32)
        mask = np.full((NS, 6), -30000.0, np.float32)
        ropeC = np.ones((64, T), np.float32)
        ropeS = np.zeros((64, T), np.float32)
        ctx_ckv = np.zeros((L, 128, 4, 256), np.float32)
        ctx_kr = np.zeros((L, 64, 256), np.float32)
        ginit = np.zeros((L, 2, 4, 128, 128), np.float32)
        sinit = np.zeros((L, 2, 8, 128, 64), np.float32)
        if samp is None:
            for s in range(NS):
                mask[s, s] = 0.0
        else:
            mask[0, 0] = 0.0
            mask[1:, 1:] = 0.0
            tab[:, TC_AL + 2:TC_AL + 8] = 1.0
            tab[:, TC_AR + 2:TC_AR + 8] = 1.0
            tab[:, TC_AS + 2:TC_AS + 5] = 1.0
            tab[:, TC_AS + 5 + 1:TC_AS + 5 + 4] = 1.0
            ropeC[:, SL:] = cos64
            ropeS[:, SL:] = sin64
            ctx_ckv = np.ascontiguousarray(f(cache_mla_ckv)[samp].transpose(0, 2, 1).reshape(L, 4, 128, 256).transpose(0, 2, 1, 3))
            ctx_kr = np.ascontiguousarray(f(cache_mla_krope)[samp].transpose(0, 2, 1))
            ginit = np.ascontiguousarray(f(state_gdn)[samp])
            sinit = np.ascontiguousarray(f(state_ssd)[samp].transpose(0, 1, 2, 4, 3))
        tab[:, TC_MASK:TC_MASK + 30] = mask.reshape(-1)[None, :]
        m = dict(shared)
        m.update(xT=xT, condT=condT, tab=tab, ropeC=ropeC, ropeS=ropeS, ctx_ckv=ctx_ckv, ctx_kr=ctx_kr,
                 gdn_init=ginit, ssd_init=sinit)
        in_maps.append(m)

    if "nc" not in _CACHE:
        _CACHE["nc"] = build_program()
    res = run_bass_kernel_spmd(_CACHE["nc"], in_maps, core_ids=list(range(8)))
    R = res.results
    _CACHE["R"] = R

    y_prompt = np.zeros((32, SL, D), np.float32)
    y_sample = np.zeros((2, 1024, D), np.float32)
    n_ckv = np.zeros((32, L, SL, 512), np.float32)
    n_kr = np.zeros((32, L, SL, 64), np.float32)
    n_gdn = np.zeros((32, L, 2, 4, 128, 128), np.float32)
    n_ssd = np.zeros((32, L, 2, 8, 64, 128), np.float32)
    for core in range(8):
        r = R[core]
        y = r["yT"].transpose(1, 0, 2).reshape(D, T).T
        ck = r["o_ckv"].transpose(0, 3, 2, 1).reshape(L, T, 512)
        kr = r["o_kr"].transpose(0, 2, 1)
        slots = [(s, 5 * core + s) for s in range(NS)] if core < 6 else [(0, 30 + core - 6)]
        for s, seq in slots:
            y_prompt[seq] = y[s * SL:(s + 1) * SL]
            n_ckv[seq] = ck[:, s * SL:(s + 1) * SL]
            n_kr[seq] = kr[:, s * SL:(s + 1) * SL]
            n_gdn[seq] = r["o_gdn"][:, s]
            n_ssd[seq] = r["o_ssd"][:, s].transpose(0, 1, 2, 4, 3)
        if core >= 6:
            y_sample[core - 6] = y[SL:]
    return (y_prompt, y_sample, n_ckv, n_kr, n_gdn, n_ssd)
```

```python
import numpy as np
import concourse.bass as bass
import concourse.mybir as mybir
from concourse.bass_utils import run_bass_kernel_spmd
from contextlib import ExitStack

F32 = mybir.dt.float32
BF16 = mybir.dt.bfloat16
AF = mybir.ActivationFunctionType
ALU = mybir.AluOpType

L = 4
D = 2048
KC = 16
NS = 5
SL = 256
T = NS * SL
NT = T // 128
TK = T + 256
FFH = 5632
FKC = 44
EPS = 1e-6
TG = [(0, 512), (512, 512), (1024, 256)]
ENGS = ("tensor", "vector", "scalar", "gpsimd", "sync")

OC_CQ, OC_CKV, OC_KR = 0, 6, 10
OC_GQ, OC_GK, OC_GV, OC_GZ = 11, 15, 19, 23
OC_SZ, OC_SX, OC_SB, OC_SC = 27, 31, 35, 37
N_OC_IN = 39

PC_N1, PC_N2, PC_ADAB, PC_QN, PC_KVN, PC_QG, PC_KG = 0, 16, 32, 128, 134, 138, 140
PC_GCW, PC_SCW, PC_SCB, PC_GNG, PC_SNG, PC_SD = 142, 202, 242, 250, 251, 255
PC_TOK = 259
NPC = PC_TOK + 48
TC_MASK, TC_AL, TC_AR, TC_AS = 0, 30, 38, 46
NTC = 56
C_ID, C_ONE, C_T128F, C_T128B, C_T64F, C_T64B, C_S64F, C_S64B, C_BLK, C_ROT = range(10)
NCONST = 10


class Tile:
    def __init__(self, handle, ncells=1, name="", excl=False):
        self.h = handle
        self.name = name
        self.ncells = ncells
        self.excl = excl
        self.state = [[None, {}] for _ in range(ncells)]

    def v(self, ap=None, cells=None):
        if ap is None:
            ap = self.h[:]
        if cells is None:
            cells = range(self.ncells)
        elif isinstance(cells, int):
            cells = (cells,)
        return View(self, ap, tuple(cells))

    def __getitem__(self, idx):
        return self.v(self.h[idx])


class View:
    def __init__(self, tile, ap, cells):
        self.tile = tile
        self.ap = ap
        self.cells = cells

    def __getitem__(self, idx):
        return View(self.tile, self.ap[idx], self.cells)


class Chan:
    def __init__(self, sem, name):
        self.sem = sem
        self.count = 0
        self.name = name


class Prog:
    def __init__(self, nc, es):
        self.nc = nc
        self.es = es
        self.ops = {e: [] for e in ENGS}
        self.sem = {e: es.enter_context(nc.semaphore("s_" + e)) for e in ENGS}
        self.cnt = {e: 0 for e in ENGS}
        self.seen = {e: {} for e in ENGS}
        self.chans = []
        self.rr = 0

    def sb(self, name, shape, dt, ncells=1):
        return Tile(self.es.enter_context(self.nc.sbuf_tensor("t_" + name, list(shape), dt)), ncells, name)

    def psum(self, name, shape, dt):
        return Tile(self.es.enter_context(self.nc.psum_tensor("p_" + name, list(shape), dt)), 1, name, excl=True)

    def chan(self, name):
        c = Chan(self.es.enter_context(self.nc.semaphore("c_" + name)), name)
        self.chans.append(c)
        return c

    def _collect(self, eng, reads, writes, is_dma):
        waits = {}

        def need(tok):
            if tok is None:
                return
            sem, val, teng, small = tok
            if teng == eng and not is_dma and not small:
                return
            k = id(sem)
            if self.seen[eng].get(k, 0) >= val:
                return
            if k not in waits or waits[k][1] < val:
                waits[k] = (sem, val)

        for v in reads:
            for c in v.cells:
                need(v.tile.state[c][0])
        for v in writes:
            for c in v.cells:
                st = v.tile.state[c]
                need(st[0])
                for t in st[1].values():
                    need(t)
        for k, (sem, val) in waits.items():
            self.seen[eng][k] = val
        return list(waits.values())

    def _split(self, reads, writes):
        r2, w2 = [], list(writes)
        for v in reads:
            (w2 if v.tile.excl else r2).append(v)
        return r2, w2

    def op(self, eng, fn, reads=(), writes=()):
        reads, writes = self._split(reads, writes)
        waits = self._collect(eng, reads, writes, False)
        self.cnt[eng] += 1
        small = False
        if eng != "tensor":
            for v in writes:
                n = 1
                for d_ in v.ap.shape[1:]:
                    n *= d_
                if n <= 128:
                    small = True
        tok = (self.sem[eng], self.cnt[eng], eng, small)
        self.ops[eng].append((waits, fn, (self.sem[eng], 1)))
        for v in writes:
            for c in v.cells:
                v.tile.state[c][0] = tok
                v.tile.state[c][1] = {}
        for v in reads:
            for c in v.cells:
                v.tile.state[c][1][eng] = tok
        return tok

    def dma(self, eng, chan, out, in_, out_view=None, in_view=None):
        reads = [in_view] if in_view is not None else []
        writes = [out_view] if out_view is not None else []
        waits = self._collect(eng, reads, writes, True)
        chan.count += 16
        tok = (chan.sem, chan.count, None, False)
        self.ops[eng].append((waits, lambda e: e.dma_start(out=out, in_=in_), (chan.sem, 16)))
        for v in writes:
            for c in v.cells:
                v.tile.state[c][0] = tok
                v.tile.state[c][1] = {}
        for v in reads:
            for c in v.cells:
                v.tile.state[c][1]["dma_" + chan.name] = tok
        return tok

    def finish(self):
        fin = [(c.sem, c.count) for c in self.chans if c.count > 0]
        self.ops["sync"].append((fin, None, None))
        with self.nc.Block() as block:
            def mk(ename):
                def body(e):
                    for waits, fn, inc in self.ops[ename]:
                        for sem, val in waits:
                            e.wait_ge(sem, val)
                        if fn is not None:
                            fn(e).then_inc(inc[0], inc[1])
                return body
            block.tensor(mk("tensor"))
            block.vector(mk("vector"))
            block.scalar(mk("scalar"))
            block.gpsimd(mk("gpsimd"))
            block.sync(mk("sync"))

    def mm(self, out, lhsT, rhs, start=True, stop=True):
        return self.op("tensor", lambda e: e.matmul(out.ap, lhsT.ap, rhs.ap, start=start, stop=stop),
                       reads=[lhsT, rhs], writes=[out])

    def transpose(self, out, in_, ident):
        return self.op("tensor", lambda e: e.transpose(out.ap, in_.ap, ident.ap), reads=[in_, ident], writes=[out])

    def act(self, out, in_, func, bias=None, scale=None):
        reads = [in_]
        kw = {}
        for nm, val in (("bias", bias), ("scale", scale)):
            if val is None:
                continue
            if isinstance(val, View):
                reads.append(val)
                kw[nm] = val.ap
            else:
                kw[nm] = val
        return self.op("scalar", lambda e: e.activation(out.ap, in_.ap, func, **kw), reads=reads, writes=[out])

    def ts(self, out, in0, s1, s2, op0, op1=None):
        reads = [in0]
        a1, a2 = s1, s2
        if isinstance(s1, View):
            reads.append(s1)
            a1 = s1.ap
        if isinstance(s2, View):
            reads.append(s2)
            a2 = s2.ap
        if op1 is None:
            return self.op("vector", lambda e: e.tensor_scalar(out.ap, in0.ap, a1, None, op0), reads=reads, writes=[out])
        return self.op("vector", lambda e: e.tensor_scalar(out.ap, in0.ap, a1, a2, op0, op1), reads=reads, writes=[out])

    def tt(self, out, in0, in1, op):
        return self.op("vector", lambda e: e.tensor_tensor(out.ap, in0.ap, in1.ap, op), reads=[in0, in1], writes=[out])

    def stt(self, out, in0, s, in1, op0, op1):
        reads = [in0, in1]
        a = s
        if isinstance(s, View):
            reads.append(s)
            a = s.ap
        return self.op("vector", lambda e: e.scalar_tensor_tensor(out.ap, in0.ap, a, in1.ap, op0, op1),
                       reads=reads, writes=[out])

    def copy(self, out, in_, eng=None):
        if eng is None:
            self.rr ^= 1
            eng = "scalar" if self.rr else "vector"
        if eng == "scalar":
            return self.op(eng, lambda e: e.copy(out.ap, in_.ap), reads=[in_], writes=[out])
        return self.op(eng, lambda e: e.tensor_copy(out.ap, in_.ap), reads=[in_], writes=[out])

    def ascale(self, out, in_, scale):
        return self.act(out, in_, AF.Identity, scale=scale)

    def memset(self, out, val):
        return self.op("vector", lambda e: e.memset(out.ap, val), reads=[], writes=[out])

    def recip(self, out, in_):
        return self.op("vector", lambda e: e.reciprocal(out.ap, in_.ap), reads=[in_], writes=[out])


def build_program(n_layers=None, debug=False):
    n_layers = L if n_layers is None else n_layers
    nc = bass.Bass("TRN2", target_bir_lowering=False)

    def din(name, shape):
        return nc.dram_tensor(name, list(shape), F32, kind="ExternalInput").ap()

    def dout(name, shape):
        return nc.dram_tensor(name, list(shape), F32, kind="ExternalOutput").ap()

    d_x = din("xT", [128, KC, T])
    d_cond = din("condT", [128, KC, NS])
    d_tab = din("tab", [128, NTC])
    d_ropec = din("ropeC", [64, T])
    d_ropes = din("ropeS", [64, T])
    d_const = din("consts", [128, NCONST, 128])
    d_pl = din("pl", [L, 128, NPC])
    d_ctxckv = din("ctx_ckv", [L, 128, 4, 256])
    d_ctxkr = din("ctx_kr", [L, 64, 256])
    d_ginit = din("gdn_init", [L, 2, 4, 128, 128])
    d_sinit = din("ssd_init", [L, 2, 8, 128, 64])
    d_ada = din("ada_w", [L, 96, 128, KC, 128])
    d_win = din("w_in", [L, N_OC_IN, 128, KC, 128])
    d_wsm = din("w_small", [L, 128, KC, 32])
    d_wuq = din("w_uq", [L, 8, 128, 6, 192])
    d_wukv = din("w_ukv", [L, 8, 128, 4, 256])
    d_wout = din("w_out", [L, 16, 128, KC, 128])
    d_wgu = din("w_gu", [L, 88, 128, KC, 128])
    d_wdn = din("w_dn", [L, 16, 128, FKC, 128])

    o_y = dout("yT", [128, KC, T])
    o_ckv = dout("o_ckv", [L, 128, 4, T])
    o_kr = dout("o_kr", [L, 64, T])
    o_gdn = dout("o_gdn", [L, NS, 2, 4, 128, 128])
    o_ssd = dout("o_ssd", [L, NS, 2, 8, 128, 64])
    o_dbg = nc.dram_tensor("o_dbg", [128, KC, T], BF16, kind="ExternalOutput").ap() if debug else None
    o_dbg2 = dout("o_dbg2", [128, 1024]) if debug else None
    u_scr = nc.dram_tensor("u_scr", [N_OC_IN, 128, T], F32, kind="Internal").ap()

    with ExitStack() as es:
        P = Prog(nc, es)
        U = Tile(u_scr, N_OC_IN, "u_scr")

        x = P.sb("x", [128, KC, T], F32, KC)
        hraw = P.sb("hraw", [128, KC * T // 2], F32, KC * 5)
        hb_ap = hraw.h[:].bitcast(BF16).rearrange("p (k t) -> p k t", t=T)
        mixed = P.sb("mixed", [128, KC, T], BF16, KC)
        ring = [P.sb("ring%d" % i, [128, 2048], BF16) for i in range(3)]
        ring_ch = [P.chan("ring%d" % i) for i in range(3)]
        fa = [P.sb("fa%d" % i, [128, TK if i < 3 else T], F32, 3) for i in range(4)]
        fa_ch = [P.chan("fa%d" % i) for i in range(4)]
        consts = P.sb("consts", [128, NCONST, 128], F32)
        id_b = P.sb("id_b", [128, 128], BF16)
        one_b = P.sb("one_b", [128, 128], BF16)
        pl = P.sb("pl", [128, NPC], F32)
        tab = P.sb("tab", [128, NTC], F32)
        sc_b = P.sb("sc_b", [128, KC, NS], BF16)
        mods = P.sb("mods", [128, 96, NS], F32)
        coef = P.sb("coef", [128, 2, KC, NS], F32)
        tokp = P.sb("tokp", [128, NT, 32], F32)
        col = [P.sb("col%d" % i, [128, 16], F32) for i in range(6)]
        epsc = P.sb("epsc", [128, 2], F32)
        pad2 = [P.sb("pad%d" % i, [128, 128], BF16) for i in range(2)]
        ptile = P.sb("ptile", [128, 2 * SL], BF16, 2)
        bank = [P.psum("bank%d" % i, [128, 512], F32) for i in range(7)]
        bankT = P.psum("bankT", [128, 1024], BF16)
        ch_consts, ch_tab, ch_rc, ch_rs, ch_ini = [P.chan(n) for n in ("consts", "tab", "rc", "rs", "ini")]
        ch_x = P.chan("x")
        ch_pl = P.chan("pl")
        ch_out = [P.chan("out%d" % i) for i in range(4)]
        ch_stg = P.chan("stg")
        ch_stg2 = [P.chan("stg0"), P.chan("stg1")]
        ch_ini2 = [P.chan("ini0"), P.chan("ini1")]
        ch_sts4 = [P.chan("sts%d" % i) for i in range(4)]
        ch_ini4 = [P.chan("ini4_%d" % i) for i in range(4)]
        ch_sts = P.chan("sts")
        state = {"bank": 0, "ring": 0, "uw": 0, "out": 0, "sm": 0, "smb": 0}
        reserved = set()

        class HView:
            pass
        def hv(kc, t0=0, t1=T, rows=None):
            e0, e1 = kc * T + t0, kc * T + t1
            cells = tuple(range(e0 // 256, (e1 - 1) // 256 + 1))
            ap = hb_ap[:, kc, t0:t1] if rows is None else hb_ap[rows[0]:rows[1], kc, t0:t1]
            return View(hraw, ap, cells)

        def hflat(kc0, nk, n, rows=None):
            e0 = kc0 * T
            cells = tuple(range(e0 // 256, (e0 + n - 1) // 256 + 1))
            ap = hb_ap[:, kc0:kc0 + nk, :].rearrange("p a t -> p (a t)")[:, 0:n]
            if rows is not None:
                ap = hb_ap[rows[0]:rows[1], kc0:kc0 + nk, :].rearrange("p a t -> p (a t)")[:, 0:n]
            return View(hraw, ap, cells)

        def smt(i):
            return View(hraw, hraw.h[:, i * 128:(i + 1) * 128], (i,))

        class _T:
            def __init__(self, i0, n=1):
                self.vw = View(hraw, hraw.h[:, i0 * 128:(i0 + n) * 128], tuple(range(i0, i0 + n)))
                self.h = hraw.h[:, i0 * 128:(i0 + n) * 128]
            def v(self):
                return self.vw
            def __getitem__(self, idx):
                return self.vw[idx]
        S_g, S_s, ini_t = _T(38), _T(38), _T(39)
        gtok_v = View(hraw, hraw.h[:, 70 * 128:70 * 128 + NT * 8].rearrange("p (t c) -> p t c", c=8), (70,))
        btok_v = View(hraw, hraw.h[:, 71 * 128:71 * 128 + NT * 8].rearrange("p (t c) -> p t c", c=8), (71,))
        dtok_v = View(hraw, hraw.h[:, 20 * 128:20 * 128 + NT * 16].rearrange("p (t c) -> p t c", c=16), (20, 21))
        atok_v = View(hraw, hraw.h[:, 22 * 128:22 * 128 + NT * 16].rearrange("p (t c) -> p t c", c=16), (22, 23))

        def smbt(i):
            e0 = 14 * T + i * 128
            return View(hraw, hb_ap[:, 14:16, :].rearrange("p a t -> p (a t)")[:, i * 128:(i + 1) * 128], (e0 // 256,))

        def nbank(pool=None):
            if pool is not None:
                pool[0] = (pool[0] + 1) % (len(pool) - 1)
                return bank[pool[1 + pool[0]]].v()
            while True:
                state["bank"] = (state["bank"] + 1) % 7
                if state["bank"] not in reserved:
                    return bank[state["bank"]].v()

        def nsm():
            state["sm"] = (state["sm"] + 1) % 6
            return smt(32 + state["sm"])

        def nsmb():
            state["smb"] = (state["smb"] + 1) % 12
            return smbt(state["smb"])

        def cst(i):
            return consts[:, i, :]

        ring_v = [(ring[i].v(), ring_ch[i]) for i in range(3)]
        extra_v = {k: (View(fa[k], fa[k].h[:].bitcast(BF16)[:, 0:2048], (0, 1, 2)), fa_ch[k]) for k in (0, 1, 2, 3)}
        state["extra"] = ()

        def wload(src_ap, ncols):
            slots = ring_v + [extra_v[k] for k in state["extra"]]
            i = state["ring"] % len(slots)
            state["ring"] += 1
            slot, ch = slots[i]
            P.dma("gpsimd", ch, slot.ap[:, 0:ncols], src_ap, out_view=slot)
            return slot

        def uload(fi, src_ap, cells, n=T):
            P.dma("sync", fa_ch[fi], fa[fi].h[:, 0:n], src_ap, out_view=fa[fi].v(), in_view=U.v(src_ap, cells))

        P.dma("sync", ch_consts, consts.h[:], d_const, out_view=consts.v())
        P.dma("sync", ch_tab, tab.h[:], d_tab, out_view=tab.v())
        P.dma("sync", fa_ch[0], fa[0].h[:, 0:KC * NS], d_cond.rearrange("p k s -> p (k s)"), out_view=fa[0].v())
        P.dma("sync", ch_x, x.h[:], d_x, out_view=x.v())
        P.copy(id_b.v(), cst(C_ID), "vector")
        P.copy(one_b.v(), cst(C_ONE), "vector")
        P.memset(epsc[:, 0:1], EPS)
        P.memset(epsc[:, 1:2], 1.0)
        P.act(sc_b.v(sc_b.h[:].rearrange("p k s -> p (k s)")), fa[0][:, 0:KC * NS], AF.Silu)
        for p_ in pad2:
            P.memset(p_.v(), 0.0)

        def sumsq_rstd(chunks, n, dim, out_f32, sq_fn):
            for (c0, cn) in [(a_, min(512, n - a_)) for a_ in range(0, n, 512)]:
                b_ = nbank()
                for i, cv in enumerate(chunks):
                    kp = cv.ap.shape[0]
                    sq = sq_fn(i, cn)
                    P.act(sq[0:kp, :], cv[:, c0:c0 + cn], AF.Square)
                    P.mm(b_[:, 0:cn], one_b[0:kp, :], sq[0:kp, :], start=(i == 0), stop=(i == len(chunks) - 1))
                P.act(out_f32[:, c0:c0 + cn], b_[:, 0:cn], AF.Sqrt, bias=epsc[:, 0:1], scale=1.0 / dim)
                P.recip(out_f32[:, c0:c0 + cn], out_f32[:, c0:c0 + cn])

        def sq_mixed(i, cn):
            return mixed.v(mixed.h[:, 14 + (i % 2), 0:cn], 14 + (i % 2))

        def sq_h(i, cn):
            return hv(13, (i % 2) * 512, (i % 2) * 512 + cn)

        def modnorm(which):
            rstd = fa[3]
            sumsq_rstd([x.v(x.h[:, kc, :], kc) for kc in range(KC)], T, D, rstd, sq_mixed)
            shift_oc = 0 if which == 0 else 48
            for kc in range(KC):
                tmp = fa[kc % 2]
                P.tt(tmp[:, 0:T], x.v(x.h[:, kc, :], kc), rstd[:, 0:T], ALU.mult)
                for s_ in range(NS):
                    P.act(hv(kc, s_ * SL, (s_ + 1) * SL), tmp[:, s_ * SL:(s_ + 1) * SL], AF.Identity,
                          bias=mods[:, shift_oc + kc, s_:s_ + 1], scale=coef[:, which, kc, s_:s_ + 1])

        def dense(src_fn, n_oc, kcn, rhs_fn, evac):
            for oc in range(n_oc):
                slot = wload(src_fn(oc), kcn * 128)
                for gi, (t0, tn) in enumerate(TG):
                    b_ = nbank()
                    for kc in range(kcn):
                        P.mm(b_[:, 0:tn], slot[:, kc * 128:(kc + 1) * 128], rhs_fn(kc, t0, t0 + tn),
                             start=(kc == 0), stop=(kc == kcn - 1))
                    evac(oc, gi, (t0, tn), b_)

        def evac_res(gate_oc):
            def f_(oc, gi, tg, b_):
                t0, tn = tg
                for s_ in range(NS):
                    a0, a1 = max(t0, s_ * SL), min(t0 + tn, (s_ + 1) * SL)
                    if a1 <= a0:
                        continue
                    P.stt(x.v(x.h[:, oc, a0:a1], oc), b_[:, a0 - t0:a1 - t0], mods[:, gate_oc + oc, s_:s_ + 1],
                          x.v(x.h[:, oc, a0:a1], oc), ALU.mult, ALU.add)
            return f_

        def rope(v64, tmp64):
            for (t0, tn) in TG:
                b_ = nbank()
                P.mm(b_[0:64, 0:tn], consts[0:64, C_ROT, 0:64], v64[:, t0:t0 + tn])
                P.tt(tmp64[:, 0:tn], b_[0:64, 0:tn], mixed.v(mixed.h[0:64, 15, t0:t0 + tn], 15), ALU.mult)
                P.tt(v64[:, t0:t0 + tn], v64[:, t0:t0 + tn], mixed.v(mixed.h[0:64, 14, t0:t0 + tn], 14), ALU.mult)
                P.tt(v64[:, t0:t0 + tn], v64[:, t0:t0 + tn], tmp64[:, 0:tn], ALU.add)

        def ada_mm(l_, ocs):
            mb_ = bank[0].v()
            for oc in ocs:
                slot = wload(d_ada[l_, oc].rearrange("p k c -> p (k c)"), KC * 128)
                for kc in range(KC):
                    P.mm(mb_[:, oc * NS:(oc + 1) * NS], slot[:, kc * 128:(kc + 1) * 128], sc_b[:, kc, :],
                         start=(kc == 0), stop=(kc == KC - 1))

        for l in range(n_layers):
            P.dma("sync", ch_pl, pl.h[:], d_pl[l], out_view=pl.v())
            mb = bank[0].v()
            if l == 0:
                state["extra"] = (0, 1, 2, 3)
                reserved.add(0)
                ada_mm(0, range(96))
            for s_ in range(NS):
                P.tt(mods[:, :, s_], View(mb.tile, mb.ap[:, 0:96 * NS].rearrange("p (o s) -> p o s", s=NS)[:, :, s_], mb.cells),
                     pl[:, PC_ADAB:PC_ADAB + 96], ALU.add)
            reserved.discard(0)
            for which, (scale_oc, pcg) in enumerate(((16, PC_N1), (64, PC_N2))):
                for s_ in range(NS):
                    P.stt(coef[:, which, :, s_], mods[:, scale_oc:scale_oc + KC, s_], 1.0, pl[:, pcg:pcg + KC], ALU.add, ALU.mult)

            state["extra"] = ()
            modnorm(0)
            state["extra"] = (2, 3)

            def evac_in(oc, gi, tg, b_):
                t0, tn = tg
                i = state["uw"] % 2
                state["uw"] += 1
                P.copy(fa[i][:, 0:tn], b_[:, 0:tn])
                P.dma("sync", fa_ch[i], u_scr[oc, :, t0:t0 + tn], fa[i].h[:, 0:tn],
                      out_view=U.v(u_scr[oc, :, t0:t0 + tn], oc), in_view=fa[i].v())
            dense(lambda oc: d_win[l, oc].rearrange("p k c -> p (k c)"), N_OC_IN, KC, hv, evac_in)

            state["extra"] = ()
            wsm = wload(d_wsm[l].rearrange("p k c -> p (k c)"), KC * 32)
            for tt_ in range(NT):
                b_ = nbank()
                for kc in range(KC):
                    P.mm(b_[:, 0:32], hv(kc, tt_ * 128, (tt_ + 1) * 128), wsm[:, kc * 32:(kc + 1) * 32], start=(kc == 0), stop=(kc == KC - 1))
                P.copy(tokp[:, tt_, :], b_[:, 0:32], "vector")
            P.dma("gpsimd", ch_rc, mixed.h[0:64, 14, :], d_ropec, out_view=mixed.v(mixed.h[0:64, 14, :], 14))
            P.dma("gpsimd", ch_rs, mixed.h[0:64, 15, :], d_ropes, out_view=mixed.v(mixed.h[0:64, 15, :], 15))
            rstd = fa[3]
            for c in range(6):
                uload(c % 2, u_scr[OC_CQ + c], OC_CQ + c)
                P.copy(hv(c), fa[c % 2][:, 0:T])
            sumsq_rstd([hv(c) for c in range(6)], T, 768.0, rstd, sq_h)
            for c in range(6):
                P.stt(mixed.v(mixed.h[:, 8 + c, :], 8 + c), hv(c), pl[:, PC_QN + c:PC_QN + c + 1], rstd[:, 0:T],
                      ALU.mult, ALU.mult)
            cqn = lambda kc, t0, t1: mixed.v(mixed.h[:, 8 + kc, t0:t1], 8 + kc)
            ckvn = lambda c, t0, t1: View(hraw, hb_ap[:, 8:13, :].rearrange("p a t -> p (a t)")[:, c * TK + t0:c * TK + t1],
                                          tuple(range((8 * T + c * TK + t0) // 256, (8 * T + c * TK + t1 - 1) // 256 + 1)))
            for c in range(4):
                uload(c % 2, u_scr[OC_CKV + c], OC_CKV + c)
                P.copy(hv(c), fa[c % 2][:, 0:T])
            sumsq_rstd([hv(c) for c in range(4)], T, 512.0, rstd, sq_h)
            for c in range(4):
                uload(c % 2, u_scr[OC_CKV + c], OC_CKV + c)
                P.stt(fa[c % 2][:, 0:T], fa[c % 2][:, 0:T], pl[:, PC_KVN + c:PC_KVN + c + 1], rstd[:, 0:T], ALU.mult, ALU.mult)
                P.copy(ckvn(c, 0, T), fa[c % 2][:, 0:T])
                P.dma("sync", fa_ch[c % 2], o_ckv[l, :, c, :], fa[c % 2].h[:, 0:T], in_view=fa[c % 2].v())
                P.dma("sync", fa_ch[2], fa[2].h[:, 0:256], d_ctxckv[l, :, c, :], out_view=fa[2].v())
                P.copy(ckvn(c, T, TK), fa[2][:, 0:256])
            kr = fa[2]
            P.dma("sync", fa_ch[2], kr.h[:, 0:T], u_scr[OC_KR], out_view=kr.v(), in_view=U.v(u_scr[OC_KR], OC_KR))
            P.dma("sync", ch_out[0], o_kr[l], kr.h[0:64, 0:T], in_view=kr.v())
            P.dma("sync", fa_ch[2], kr.h[0:64, T:TK], d_ctxkr[l], out_view=kr.v())
            krsq = hflat(14, 2, TK, rows=(0, 64))
            P.act(krsq, kr[0:64, 0:TK], AF.Square)
            sc = 192.0 ** -0.5

            for hd in range(8):
                slot = wload(d_wuq[l, hd].rearrange("p k c -> p (k c)"), 6 * 192)
                qn, qr = fa[0], fa[1]
                for (t0, tn) in TG:
                    b_ = nbank()
                    for kc in range(6):
                        P.mm(b_[:, 0:tn], slot[:, kc * 192:kc * 192 + 128], cqn(kc, t0, t0 + tn), start=(kc == 0), stop=(kc == 5))
                    P.copy(qn[:, t0:t0 + tn], b_[:, 0:tn])
                    b_ = nbank()
                    for kc in range(6):
                        P.mm(b_[0:64, 0:tn], slot[:, kc * 192 + 128:kc * 192 + 192], cqn(kc, t0, t0 + tn), start=(kc == 0), stop=(kc == 5))
                    P.copy(qr[0:64, t0:t0 + tn], b_[0:64, 0:tn])
                sumsq_rstd([qn[:, 0:T], qr[0:64, 0:T]], T, 192.0, rstd, sq_h)
                P.stt(qn[:, 0:T], qn[:, 0:T], pl[:, PC_QG:PC_QG + 1], rstd[:, 0:T], ALU.mult, ALU.mult)
                P.ts(hv(0), qn[:, 0:T], sc, None, ALU.mult)
                P.stt(qr[0:64, 0:T], qr[0:64, 0:T], pl[0:64, PC_QG + 1:PC_QG + 2], rstd[0:64, 0:T], ALU.mult, ALU.mult)
                rope(qr[0:64, 0:T], fa[3][0:64, 0:512])
                P.ts(hv(1, rows=(0, 64)), qr[0:64, 0:T], sc, None, ALU.mult)
                slot = wload(d_wukv[l, hd].rearrange("p k c -> p (k c)"), 4 * 256)
                kn = fa[0]
                KG = [(0, 512), (512, 512), (1024, 512)]
                for (t0, tn) in KG:
                    b_ = nbank()
                    for kc in range(4):
                        P.mm(b_[:, 0:tn], slot[:, kc * 256:kc * 256 + 128], ckvn(kc, t0, t0 + tn), start=(kc == 0), stop=(kc == 3))
                    P.copy(kn[:, t0:t0 + tn], b_[:, 0:tn])
                vt = lambda a0, a1: View(hraw, hb_ap[:, 2:4, :].rearrange("p a t -> p (a t)")[:, a0:a1],
                                         tuple(range((2 * T + a0) // 256, (2 * T + a1 - 1) // 256 + 1)))
                for kt in range(TK // 128):
                    if kt % 4 == 0:
                        b_ = nbank()
                    for kc in range(4):
                        P.mm(b_[:, (kt % 4) * 128:(kt % 4 + 1) * 128], ckvn(kc, kt * 128, (kt + 1) * 128),
                             slot[:, kc * 256 + 128:kc * 256 + 256], start=(kc == 0), stop=(kc == 3))
                    if kt % 4 == 3:
                        P.copy(vt((kt - 3) * 128, (kt + 1) * 128), b_[:, 0:512])
                krstd = fa[1]
                for (t0, tn) in KG:
                    b_ = nbank()
                    sq = sq_h(0, tn)
                    P.act(sq, kn[:, t0:t0 + tn], AF.Square)
                    P.mm(b_[:, 0:tn], one_b.v(), sq, start=True, stop=False)
                    P.mm(b_[:, 0:tn], one_b[0:64, :], krsq[:, t0:t0 + tn], start=False, stop=True)
                    P.act(krstd[:, t0:t0 + tn], b_[:, 0:tn], AF.Sqrt, bias=epsc[:, 0:1], scale=1.0 / 192)
                    P.recip(krstd[:, t0:t0 + tn], krstd[:, t0:t0 + tn])
                knb = lambda a0, a1: View(hraw, hb_ap[:, 4:6, :].rearrange("p a t -> p (a t)")[:, a0:a1],
                                          tuple(range((4 * T + a0) // 256, (4 * T + a1 - 1) // 256 + 1)))
                krb = lambda a0, a1: View(hraw, hb_ap[0:64, 6:8, :].rearrange("p a t -> p (a t)")[:, a0:a1],
                                          tuple(range((6 * T + a0) // 256, (6 * T + a1 - 1) // 256 + 1)))
                P.stt(knb(0, TK), kn[:, 0:TK], pl[:, PC_KG:PC_KG + 1], krstd[:, 0:TK], ALU.mult, ALU.mult)
                krh = fa[0]
                P.stt(krh[0:64, 0:TK], kr[0:64, 0:TK], pl[0:64, PC_KG + 1:PC_KG + 2], krstd[0:64, 0:TK], ALU.mult, ALU.mult)
                rope(krh[0:64, 0:T], fa[3][0:64, 0:512])
                P.copy(krb(0, TK), krh[0:64, 0:TK], "vector")
                reserved.update((0, 1))
                for s_ in range(NS):
                    blocks = [0] if s_ == 0 else [1, 2, 3, 4, 5]
                    q0 = s_ * SL
                    ob, db = bank[0].v(), bank[1].v()
                    nk = len(blocks) * 2
                    for bi, blk in enumerate(blocks):
                        for half in range(2):
                            i_ = bi * 2 + half
                            k0 = blk * SL + half * 128
                            sb_ = nbank()
                            P.mm(sb_[:, 0:SL], knb(k0, k0 + 128), hv(0, q0, q0 + SL), start=True, stop=False)
                            P.mm(sb_[:, 0:SL], krb(k0, k0 + 128), hv(1, q0, q0 + SL, rows=(0, 64)), start=False, stop=True)
                            pt = ptile.v(ptile.h[:, (i_ % 2) * SL:(i_ % 2 + 1) * SL], i_ % 2)
                            mc = TC_MASK + s_ * 6 + blk
                            P.act(pt, sb_[:, 0:SL], AF.Exp, bias=tab[:, mc:mc + 1])
                            P.mm(ob[:, 0:SL], vt(k0, k0 + 128), pt, start=(i_ == 0), stop=(i_ == nk - 1))
                            P.mm(db[:, 0:SL], one_b.v(), pt, start=(i_ == 0), stop=(i_ == nk - 1))
                    rc = fa[1]
                    P.recip(rc[:, 0:SL], db[:, 0:SL])
                    P.tt(mixed.v(mixed.h[:, hd, q0:q0 + SL], hd), ob[:, 0:SL], rc[:, 0:SL], ALU.mult)
                reserved.difference_update((0, 1))

            tk = PC_TOK
            P.act(col[0][:, 0:8], pl[:, tk:tk + 8], AF.Exp)
            for tt_ in range(NT):
                P.act(btok_v[:, tt_, :], tokp[:, tt_, 0:8], AF.Sigmoid)
                P.tt(gtok_v[:, tt_, :], tokp[:, tt_, 8:16], pl[:, tk + 8:tk + 16], ALU.add)
                P.act(gtok_v[:, tt_, :], gtok_v[:, tt_, :], AF.Exp)
                P.act(gtok_v[:, tt_, :], gtok_v[:, tt_, :], AF.Ln, bias=epsc[:, 1:2])
                P.stt(gtok_v[:, tt_, :], gtok_v[:, tt_, :], -1.0, col[0][:, 0:8], ALU.mult, ALU.mult)

            def conv_silu(oc, wcol, bias_col, fo):
                xp = fa[2]
                xpv = View(xp, xp.h[:, 0:NS * 260].rearrange("p (s t) -> p s t", t=260), (0,))
                P.dma("sync", fa_ch[2], xpv.ap[:, :, 2:258], u_scr[oc].rearrange("p (s t) -> p s t", t=SL),
                      out_view=xp.v(), in_view=U.v(u_scr[oc], oc))
                P.memset(xpv[:, 0, 0:2], 0.0)
                P.memset(xpv[:, NS - 1, 258:260], 0.0)
                al = View(tab, tab.h[:, TC_AL:TC_AL + 8].rearrange("p (s t) -> p s t", t=2), (0,))
                ar = View(tab, tab.h[:, TC_AR:TC_AR + 8].rearrange("p (s t) -> p s t", t=2), (0,))
                P.tt(xpv[:, 1:NS, 0:2], xpv[:, 0:NS - 1, 256:258], al, ALU.mult)
                P.tt(xpv[:, 0:NS - 1, 258:260], xpv[:, 1:NS, 2:4], ar, ALU.mult)
                o3 = View(fa[fo], fa[fo].h[:, 0:T].rearrange("p (s t) -> p s t", t=SL), (0,))
                P.ts(o3, xpv[:, :, 0:256], pl[:, wcol:wcol + 1], None, ALU.mult)
                for j in range(1, 5):
                    P.stt(o3, xpv[:, :, j:j + 256], pl[:, wcol + j:wcol + j + 1], o3, ALU.mult, ALU.add)
                if bias_col is None:
                    P.act(fa[fo][:, 0:T], fa[fo][:, 0:T], AF.Silu)
                else:
                    P.act(fa[fo][:, 0:T], fa[fo][:, 0:T], AF.Silu, bias=pl[:, bias_col:bias_col + 1])

            def silu_load(oc, fo):
                uload(fo, u_scr[oc], oc)
                P.act(fa[fo][:, 0:T], fa[fo][:, 0:T], AF.Silu)

            def transpose_tiles(src_fn, dst_fn):
                for tt_ in range(NT):
                    bt = bankT.v()
                    P.transpose(bt[:, 0:128], src_fn(tt_), id_b.v())
                    P.copy(dst_fn(tt_), bt[:, 0:128])

            QT, KT, VT, KTOK, VTOK = 8, 9, 10, 11, 12
            for hd in range(4):
                rs = fa[3]
                conv_silu(OC_GQ + hd, PC_GCW + hd * 5, None, 0)
                sumsq_rstd([fa[0][:, 0:T]], T, 1.0, rs, sq_h)
                P.stt(hv(QT), fa[0][:, 0:T], 128.0 ** -0.5, rs[:, 0:T], ALU.mult, ALU.mult)
                conv_silu(OC_GK + hd, PC_GCW + (4 + hd) * 5, None, 0)
                sumsq_rstd([fa[0][:, 0:T]], T, 1.0, rs, sq_h)
                P.tt(hv(KT), fa[0][:, 0:T], rs[:, 0:T], ALU.mult)
                conv_silu(OC_GV + hd, PC_GCW + (8 + hd) * 5, None, 0)
                P.copy(hv(VT), fa[0][:, 0:T])
                oacc = fa[1]
                transpose_tiles(lambda t_: hv(KT, t_ * 128, (t_ + 1) * 128), lambda t_: hv(KTOK, t_ * 128, (t_ + 1) * 128))
                transpose_tiles(lambda t_: hv(VT, t_ * 128, (t_ + 1) * 128), lambda t_: hv(VTOK, t_ * 128, (t_ + 1) * 128))
                oacc_b = fa[2]

                def gdn_chain(dr, hd=hd):
                    tri, stri = (C_T64F, C_S64F) if dr == 0 else (C_T64B, C_S64B)
                    ci = dr * 4 + hd
                    base = dr * 17
                    g_bc, b_bc, Dm, egr, brow, egl, Nm, NT_, qkT, qg, vnew = [smt(base + i) for i in range(11)]
                    kbg, vb_, kdec, wT, uu = g_bc, b_bc, brow, Nm, NT_
                    pool = [smt(base + 11 + i) for i in range(6)]
                    pstate = [0]
                    bkp = [0, 0, 1, 2] if dr == 0 else [0, 3, 4, 5]

                    def npool():
                        pstate[0] = (pstate[0] + 1) % 6
                        return pool[pstate[0]]
                    Sg = _T(34 + dr)
                    oa = fa[1] if dr == 0 else oacc_b
                    cc = col[1 + dr]
                    P.memset(Sg.v(), 0.0)
                    order = list(range(NT)) if dr == 0 else list(range(NT - 1, -1, -1))
                    for tt_ in order:
                        s_ = tt_ // 2
                        first = (tt_ % 2 == 0) if dr == 0 else (tt_ % 2 == 1)
                        if first:
                            ac = TC_AS + dr * 5 + s_
                            P.ts(Sg.v(), Sg.v(), tab[:, ac:ac + 1], None, ALU.mult)
                            if (dr == 0 and s_ == 1) or (dr == 1 and s_ == 4):
                                it = _T(36 + dr)
                                P.dma("sync", ch_ini2[dr], it.h[:], d_ginit[l, dr, hd], out_view=it.v())
                                P.tt(Sg.v(), Sg.v(), it.v(), ALU.add)
                        tq = lambda c_: hv(c_, tt_ * 128, (tt_ + 1) * 128)
                        gcol = gtok_v[:, tt_, ci:ci + 1]
                        bcol = btok_v[:, tt_, ci:ci + 1]
                        P.ascale(g_bc, cst(C_ONE), gcol)
                        P.ascale(b_bc, cst(C_ONE), bcol)
                        bA = nbank(bkp)
                        P.mm(bA[:, 0:128], g_bc, cst(tri))
                        P.mm(bA[:, 128:256], b_bc, cst(C_ID))
                        P.mm(bA[:, 256:257], cst(tri), gcol)
                        P.mm(bA[:, 257:258], cst(C_BLK), gcol)
                        yield
                        P.copy(cc[:, 0:2], bA[:, 256:258], "scalar")
                        yield
                        P.act(cc[:, 2:3], cc[:, 0:1], AF.Exp)
                        yield
                        P.tt(cc[:, 3:4], cc[:, 2:3], bcol, ALU.mult)
                        P.tt(cc[:, 4:5], cc[:, 1:2], cc[:, 0:1], ALU.subtract)
                        yield
                        P.act(cc[:, 4:5], cc[:, 4:5], AF.Exp)
                        P.ts(Dm, bA[:, 0:128], cc[:, 0:1], 0.0, ALU.subtract, ALU.min)
                        yield
                        P.act(Dm, Dm, AF.Exp)
                        yield
                        P.tt(Dm, Dm, cst(tri), ALU.mult)
                        P.act(egr, bA[:, 0:128], AF.Exp)
                        yield
                        P.copy(brow, bA[:, 128:256], "scalar")
                        lastc = (63, 127) if dr == 0 else (0, 64)
                        bB = nbank(bkp)
                        P.mm(bB[:, 0:128], tq(KT), tq(KT))
                        P.mm(bB[:, 128:256], tq(KT), tq(QT))
                        yield
                        P.stt(Nm, bB[:, 0:128], -1.0, Dm, ALU.mult, ALU.mult)
                        P.tt(Nm, Nm, cst(stri), ALU.mult)
                        P.tt(Nm, Nm, brow, ALU.mult)
                        P.tt(qkT, bB[:, 128:256], Dm, ALU.mult)
                        yield
                        bC = nbank(bkp)
                        P.transpose(bC[:, 0:128], Nm, cst(C_ID))
                        yield
                        P.copy(NT_, bC[:, 0:128], "scalar")
                        pairs = [View(hraw, hraw.h[:, (base + 11 + 2 * i) * 128:(base + 13 + 2 * i) * 128],
                                      (base + 11 + 2 * i, base + 12 + 2 * i)) for i in range(2)]
                        singles = [smt(base + 15), smt(base + 16)]
                        bD = nbank(bkp)
                        P.mm(bD[:, 128:256], NT_, Nm)
                        P.mm(bD[:, 256:384], Nm, NT_)
                        yield
                        PA, AT = pairs[0], singles[0]
                        P.tt(PA[:, 0:128], Nm, cst(C_ID), ALU.add)
                        P.copy(PA[:, 128:256], bD[:, 128:256], "scalar")
                        P.copy(AT, bD[:, 256:384], "vector")
                        yield
                        for lev in range(1, 6):
                            bD = nbank(bkp)
                            if lev < 5:
                                P.mm(bD[:, 0:256], AT, PA[:, 0:256])
                                P.mm(bD[:, 256:384], PA[:, 128:256], AT)
                            else:
                                P.mm(bD[:, 0:128], AT, PA[:, 0:128])
                            yield
                            PA2, AT2 = pairs[lev % 2], singles[lev % 2]
                            P.tt(PA2[:, 0:128], PA[:, 0:128], bD[:, 0:128], ALU.add)
                            if lev < 5:
                                P.copy(PA2[:, 128:256], bD[:, 128:256], "scalar")
                                P.copy(AT2, bD[:, 256:384], "scalar")
                            yield
                            PA, AT = PA2, AT2
                        Pm = PA[:, 0:128]
                        TmT = Pm
                        P.ascale(kbg, tq(KTOK), cc[:, 3:4])
                        P.ascale(vb_, tq(VTOK), bcol)
                        P.ascale(kdec, tq(KTOK), cc[:, 4:5])
                        yield
                        bF = nbank(bkp)
                        P.mm(bF[:, 0:128], kbg, TmT)
                        P.mm(bF[:, 128:256], TmT, vb_)
                        yield
                        P.copy(wT, bF[:, 0:128], "scalar")
                        P.copy(uu, bF[:, 128:256], "scalar")
                        P.tt(qg, tq(QT), egr, ALU.mult)
                        yield
                        for c in ((0, 1) if dr == 0 else (1, 0)):
                            r = slice(c * 64, c * 64 + 64)
                            bG = nbank(bkp)
                            P.mm(bG[:, 0:128], wT, Sg.v())
                            yield
                            P.tt(vnew[r, :], uu[r, :], bG[r, 0:128], ALU.subtract)
                            P.mm(bG[:, 128:192], Sg.v(), qg[:, r], start=True, stop=False)
                            P.mm(bG[:, 128:192], vnew[r, :], qkT[r, r], start=False, stop=True)
                            P.mm(bG[:, 256:384], kdec[r, :], vnew[r, :])
                            yield
                            t0 = tt_ * 128 + c * 64
                            P.copy(oa[:, t0:t0 + 64], bG[:, 128:192], "scalar")
                            P.stt(Sg.v(), Sg.v(), egr[:, lastc[c]:lastc[c] + 1], bG[:, 256:384], ALU.mult, ALU.add)
                        last = (tt_ % 2 == 1) if dr == 0 else (tt_ % 2 == 0)
                        if last:
                            P.dma("sync", ch_stg2[dr], o_gdn[l, s_, dr, hd], Sg.h, in_view=Sg.v())

                gens = [gdn_chain(0), gdn_chain(1)]
                while gens:
                    for g_ in list(gens):
                        try:
                            next(g_)
                        except StopIteration:
                            gens.remove(g_)
                P.tt(oacc[:, 0:T], oacc[:, 0:T], oacc_b[:, 0:T], ALU.add)
                sumsq_rstd([oacc[:, 0:T]], T, 128.0, rs, sq_h)
                P.stt(oacc[:, 0:T], oacc[:, 0:T], pl[:, PC_GNG:PC_GNG + 1], rs[:, 0:T], ALU.mult, ALU.mult)
                silu_load(OC_GZ + hd, 0)
                P.tt(mixed.v(mixed.h[:, 8 + hd, :], 8 + hd), oacc[:, 0:T], fa[0][:, 0:T], ALU.mult)

            BTc, CTc, BTOK, XB, XTOK = 8, 9, 10, 11, 12
            P.act(col[5][:, 0:16], pl[:, PC_TOK + 16:PC_TOK + 32], AF.Exp)
            for tt_ in range(NT):
                P.tt(dtok_v[:, tt_, :], tokp[:, tt_, 16:32], pl[:, PC_TOK + 32:PC_TOK + 48], ALU.add)
                P.act(dtok_v[:, tt_, :], dtok_v[:, tt_, :], AF.Exp)
                P.act(dtok_v[:, tt_, :], dtok_v[:, tt_, :], AF.Ln, bias=epsc[:, 1:2])
                P.stt(atok_v[:, tt_, :], dtok_v[:, tt_, :], -1.0, col[5][:, 0:16], ALU.mult, ALU.mult)
            for grp in range(2):
                conv_silu(OC_SB + grp, PC_SCW + (4 + grp) * 5, PC_SCB + 4 + grp, 0)
                P.copy(hv(BTc), fa[0][:, 0:T])
                conv_silu(OC_SC + grp, PC_SCW + (6 + grp) * 5, PC_SCB + 6 + grp, 0)
                P.copy(hv(CTc), fa[0][:, 0:T])
                transpose_tiles(lambda t_: hv(BTc, t_ * 128, (t_ + 1) * 128), lambda t_: hv(BTOK, t_ * 128, (t_ + 1) * 128))
                for pair in range(2):
                    chn = grp * 2 + pair
                    conv_silu(OC_SX + chn, PC_SCW + chn * 5, PC_SCB + chn, 0)
                    P.copy(hv(XB), fa[0][:, 0:T])
                    transpose_tiles(lambda t_: hv(XB, t_ * 128, (t_ + 1) * 128), lambda t_: hv(XTOK, t_ * 128, (t_ + 1) * 128))
                    yacc = fa[1]
                    P.memset(yacc[:, 0:T], 0.0)
                    def ssd_chain(par, dr, chn=chn):
                        tri = C_T128F if dr == 0 else C_T128B
                        cidx = dr * 2 + par
                        bkp = [[0, 0, 1], [0, 2, 3], [0, 4, 5], [0, 6]][cidx]
                        hd = chn * 2 + par
                        ci = dr * 8 + hd
                        hc = slice(par * 64, par * 64 + 64)
                        a_bc, Lm, ear = smt(cidx * 3), smt(cidx * 3 + 1), smt(cidx * 3 + 2)
                        Ss = _T(12 + cidx)
                        scT, ceT, xdte, Sb = [smbt(cidx * 4 + i) for i in range(4)]
                        pd = smbt(16 + cidx)
                        cc = col[cidx] if cidx < 3 else col[4]
                        P.memset(pd, 0.0)
                        P.memset(Sb, 0.0)
                        P.memset(Ss.v(), 0.0)
                        order = list(range(NT)) if dr == 0 else list(range(NT - 1, -1, -1))
                        for tt_ in order:
                            s_ = tt_ // 2
                            first = (tt_ % 2 == 0) if dr == 0 else (tt_ % 2 == 1)
                            if first:
                                ac = TC_AS + dr * 5 + s_
                                P.ts(Ss[:, hc], Ss[:, hc], tab[:, ac:ac + 1], None, ALU.mult)
                                if (dr == 0 and s_ == 1) or (dr == 1 and s_ == 4):
                                    it = _T(16 + cidx)
                                    P.dma("sync", ch_ini4[cidx], it.h[:, 0:64], d_sinit[l, dr, hd], out_view=it.v())
                                    P.tt(Ss[:, hc], Ss[:, hc], it[:, 0:64], ALU.add)
                            tq = lambda c_: hv(c_, tt_ * 128, (tt_ + 1) * 128)
                            acol = atok_v[:, tt_, ci:ci + 1]
                            dcol = dtok_v[:, tt_, ci:ci + 1]
                            P.ascale(a_bc, cst(C_ONE), acol)
                            bA = nbank(bkp)
                            P.mm(bA[:, 0:128], a_bc, cst(tri))
                            P.mm(bA[:, 128:129], cst(tri), acol)
                            P.mm(bA[:, 129:130], cst(C_ONE), acol)
                            P.mm(bA[:, 256:384], tq(BTc), tq(CTc))
                            yield
                            P.copy(cc[:, 0:2], bA[:, 128:130], "scalar")
                            yield
                            P.tt(cc[:, 2:3], cc[:, 1:2], cc[:, 0:1], ALU.subtract)
                            yield
                            P.act(cc[:, 2:3], cc[:, 2:3], AF.Exp)
                            yield
                            P.tt(cc[:, 3:4], cc[:, 2:3], dcol, ALU.mult)
                            P.ts(Lm, bA[:, 0:128], cc[:, 0:1], 0.0, ALU.subtract, ALU.min)
                            yield
                            P.act(Lm, Lm, AF.Exp)
                            P.tt(Lm, Lm, cst(tri), ALU.mult)
                            P.act(ear, bA[:, 0:128], AF.Exp)
                            yield
                            P.tt(scT, bA[:, 256:384], Lm, ALU.mult)
                            P.tt(ceT, tq(CTc), ear, ALU.mult)
                            yield
                            lastc = 127 if dr == 0 else 0
                            P.ascale(pd[:, hc], tq(XTOK)[:, hc], dcol)
                            P.ascale(xdte[:, 0:64], tq(XTOK)[:, hc], cc[:, 3:4])
                            P.copy(Sb[:, hc], Ss[:, hc], "scalar")
                            yield
                            bY = nbank(bkp)
                            P.mm(bY[:, 0:128], pd, scT, start=True, stop=False)
                            P.mm(bY[:, 0:128], Sb, ceT, start=False, stop=True)
                            P.mm(bY[:, 128:192], tq(BTOK), xdte[:, 0:64])
                            yield
                            tsl = slice(tt_ * 128, (tt_ + 1) * 128)
                            P.tt(yacc[:, tsl], yacc[:, tsl], bY[:, 0:128], ALU.add)
                            P.stt(Ss[:, hc], Ss[:, hc], ear[:, lastc:lastc + 1], bY[:, 128:192], ALU.mult, ALU.add)
                            last = (tt_ % 2 == 1) if dr == 0 else (tt_ % 2 == 0)
                            if last:
                                P.dma("sync", ch_sts4[cidx], o_ssd[l, s_, dr, hd], Ss.h[:, hc], in_view=Ss.v())

                    gens = [ssd_chain(0, 0), ssd_chain(1, 0), ssd_chain(0, 1), ssd_chain(1, 1)]
                    while gens:
                        for g_ in list(gens):
                            try:
                                next(g_)
                            except StopIteration:
                                gens.remove(g_)
                    P.stt(yacc[:, 0:T], fa[0][:, 0:T], pl[:, PC_SD + chn:PC_SD + chn + 1], yacc[:, 0:T], ALU.mult, ALU.add)
                    silu_load(OC_SZ + chn, 0)
                    dst = fa[3] if pair == 0 else fa[1]
                    P.tt(dst[:, 0:T], yacc[:, 0:T], fa[0][:, 0:T], ALU.mult)
                rs = fa[2]
                sumsq_rstd([fa[3][:, 0:T], fa[1][:, 0:T]], T, 256.0, rs, sq_h)
                for pair, src in enumerate((fa[3], fa[1])):
                    chn = grp * 2 + pair
                    P.stt(mixed.v(mixed.h[:, 12 + chn, :], 12 + chn), src[:, 0:T], pl[:, PC_SNG + chn:PC_SNG + chn + 1],
                          rs[:, 0:T], ALU.mult, ALU.mult)

            if debug and l == 0:
                P.dma("sync", ch_out[2], o_dbg, mixed.h[:], in_view=mixed.v())
            state["extra"] = (0, 1, 2, 3)
            dense(lambda oc: d_wout[l, oc].rearrange("p k c -> p (k c)"), KC, KC,
                  lambda kc, t0, t1: mixed.v(mixed.h[:, kc, t0:t1], kc), evac_res(32))

            state["extra"] = ()
            modnorm(1)
            state["extra"] = (1, 2, 3)
            if l + 1 < n_layers:
                reserved.add(0)
            FB = 8
            fblocks = [(b0, min(FB, FKC - b0)) for b0 in range(0, FKC, FB)]
            for bi, (b0, bn) in enumerate(fblocks):
                base = (bi % 2) * 8
                for j in range(bn):
                    f_ = b0 + j
                    sg = wload(d_wgu[l, f_].rearrange("p k c -> p (k c)"), KC * 128)
                    su = wload(d_wgu[l, 44 + f_].rearrange("p k c -> p (k c)"), KC * 128)
                    for gi_, (t0, tn) in enumerate(TG):
                        bg, bu = nbank(), nbank()
                        for kc in range(KC):
                            P.mm(bg[:, 0:tn], sg[:, kc * 128:(kc + 1) * 128], hv(kc, t0, t0 + tn), start=(kc == 0), stop=(kc == KC - 1))
                        for kc in range(KC):
                            P.mm(bu[:, 0:tn], su[:, kc * 128:(kc + 1) * 128], hv(kc, t0, t0 + tn), start=(kc == 0), stop=(kc == KC - 1))
                        sv = fa[0].v(fa[0].h[:, (gi_ % 2) * 512:(gi_ % 2) * 512 + tn], gi_ % 2)
                        P.act(sv, bg[:, 0:tn], AF.Silu)
                        P.tt(mixed.v(mixed.h[:, base + j, t0:t0 + tn], base + j), sv, bu[:, 0:tn], ALU.mult)
                    if l + 1 < n_layers:
                        ada_mm(l + 1, range(f_ * 96 // FKC, (f_ + 1) * 96 // FKC))
                for oc in range(KC):
                    sd = wload(d_wdn[l, oc, :, b0:b0 + bn, :].rearrange("p k c -> p (k c)"), bn * 128)
                    for gi, (t0, tn) in enumerate(TG):
                        b_ = nbank()
                        for j in range(bn):
                            P.mm(b_[:, 0:tn], sd[:, j * 128:(j + 1) * 128], mixed.v(mixed.h[:, base + j, t0:t0 + tn], base + j),
                                 start=(j == 0), stop=(j == bn - 1))
                        evac_res(80)(oc, gi, (t0, tn), b_)
            state["extra"] = ()

        P.dma("sync", ch_out[1], o_y, x.h[:], in_view=x.v())
        P.finish()
    return nc


def _tile_w(w, kc=None):
    K, N = w.shape
    return np.ascontiguousarray(w.reshape(K // 128, 128, N // 128, 128).transpose(2, 1, 0, 3))


def _consts():
    c = np.zeros((NCONST, 128, 128), np.float32)
    j = np.arange(128)[:, None]
    i = np.arange(128)[None, :]
    same = (j // 64) == (i // 64)
    c[C_ID] = np.eye(128)
    c[C_ONE] = 1.0
    c[C_T128F] = (i >= j)
    c[C_T128B] = (i <= j)
    c[C_T64F] = same & (i >= j)
    c[C_T64B] = same & (i <= j)
    c[C_S64F] = same & (i > j)
    c[C_S64B] = same & (i < j)
    c[C_BLK] = same
    for f in range(16):
        for base in (0, 32):
            c[C_ROT, base + 16 + f, base + f] = -1.0
            c[C_ROT, base + f, base + 16 + f] = 1.0
    return np.ascontiguousarray(c.transpose(1, 0, 2))


def _fm(v):
    return v.reshape(-1, 128).T


_CACHE = {}


def kernel(x_prompt, x_sample, cache_mla_ckv, cache_mla_krope, state_gdn, state_ssd, c, c_ctx,
           norm1_g, norm2_g, ada_w, ada_b, w_in, w_out, mla_qnorm_g, mla_w_uq, mla_kvnorm_g,
           mla_w_ukv, mla_q_g, mla_k_g, gdn_conv_w, gdn_a_log, gdn_dt_bias, gdn_norm_g,
           ssd_conv_w, ssd_conv_b, ssd_a_log, ssd_dt_bias, ssd_d, ssd_norm_g, ffn_w_gu, ffn_w_down):
    f = lambda a: np.asarray(a, np.float32)
    x_prompt, x_sample = f(x_prompt), f(x_sample)
    adaT = np.stack([_tile_w(f(ada_w[l])) for l in range(L)])
    w_in = f(w_in)
    secs = [(0, 768), (768, 1280), ("kr", 1280), (1344, 1856), (1856, 2368), (2368, 2880), (2880, 3392),
            (3408, 3920), (3920, 4432), (4432, 4688), (4688, 4944)]
    cols = []
    for a, b in secs:
        if a == "kr":
            cols.append(np.concatenate([np.arange(1280, 1344), np.full(64, -1)]))
        else:
            cols.append(np.arange(a, b))
    cols = np.concatenate(cols)
    win_r = np.where(cols[None, None, :] >= 0, w_in[:, :, np.maximum(cols, 0)], 0.0).astype(np.float32)
    winT = np.stack([_tile_w(win_r[l]) for l in range(L)])
    small_cols = np.concatenate([np.arange(3392, 3408), np.arange(4944, 4960)])
    wsm = np.ascontiguousarray(w_in[:, :, small_cols].reshape(L, KC, 128, 32).transpose(0, 2, 1, 3))
    wuq = np.ascontiguousarray(f(mla_w_uq).reshape(L, 6, 128, 8, 192).transpose(0, 3, 2, 1, 4))
    wukv = np.ascontiguousarray(f(mla_w_ukv).reshape(L, 4, 128, 8, 256).transpose(0, 3, 2, 1, 4))
    woutT = np.stack([_tile_w(f(w_out[l])) for l in range(L)])
    wguT = np.stack([_tile_w(f(ffn_w_gu[l])) for l in range(L)])
    wdnT = np.stack([_tile_w(f(ffn_w_down[l])) for l in range(L)])
    pl = np.zeros((L, 128, NPC), np.float32)
    for l in range(L):
        pl[l, :, PC_N1:PC_N1 + 16] = _fm(f(norm1_g[l]))
        pl[l, :, PC_N2:PC_N2 + 16] = _fm(f(norm2_g[l]))
        pl[l, :, PC_ADAB:PC_ADAB + 96] = _fm(f(ada_b[l]))
        pl[l, :, PC_QN:PC_QN + 6] = _fm(f(mla_qnorm_g[l]))
        pl[l, :, PC_KVN:PC_KVN + 4] = _fm(f(mla_kvnorm_g[l]))
        pl[l, :, PC_QG] = f(mla_q_g[l])[:128]
        pl[l, :64, PC_QG + 1] = f(mla_q_g[l])[128:]
        pl[l, :, PC_KG] = f(mla_k_g[l])[:128]
        pl[l, :64, PC_KG + 1] = f(mla_k_g[l])[128:]
        gcw = f(gdn_conv_w[l])
        pl[l, :, PC_GCW:PC_GCW + 60] = gcw.reshape(5, 12, 128).transpose(2, 1, 0).reshape(128, 60)
        scw = f(ssd_conv_w[l])
        pl[l, :, PC_SCW:PC_SCW + 40] = scw.reshape(5, 8, 128).transpose(2, 1, 0).reshape(128, 40)
        pl[l, :, PC_SCB:PC_SCB + 8] = _fm(f(ssd_conv_b[l]))
        pl[l, :, PC_GNG] = f(gdn_norm_g[l])
        pl[l, :, PC_SNG:PC_SNG + 4] = _fm(f(ssd_norm_g[l]))
        pl[l, :, PC_SD:PC_SD + 4] = _fm(np.repeat(f(ssd_d[l]), 64))
        pl[l, :, PC_TOK:PC_TOK + 8] = f(gdn_a_log[l]).reshape(-1)[None, :]
        pl[l, :, PC_TOK + 8:PC_TOK + 16] = f(gdn_dt_bias[l]).reshape(-1)[None, :]
        pl[l, :, PC_TOK + 16:PC_TOK + 32] = f(ssd_a_log[l]).reshape(-1)[None, :]
        pl[l, :, PC_TOK + 32:PC_TOK + 48] = f(ssd_dt_bias[l]).reshape(-1)[None, :]
    consts = _consts()
    rows = np.repeat(np.arange(16, dtype=np.float32), 64)
    colp = np.tile(np.arange(64, dtype=np.float32), 16)
    inv = (10000.0 ** (-np.arange(16, dtype=np.float32) / 16)).astype(np.float32)
    ang = np.stack([rows[:, None] * inv, colp[:, None] * inv], 1)
    cosf = np.cos(ang).astype(np.float32)
    sinf = np.sin(ang).astype(np.float32)
    cos64 = np.concatenate([cosf[:, 0], cosf[:, 0], cosf[:, 1], cosf[:, 1]], 1).T
    sin64 = np.concatenate([sinf[:, 0], sinf[:, 0], sinf[:, 1], sinf[:, 1]], 1).T

    shared = dict(consts=consts, pl=pl, ada_w=adaT, w_in=winT, w_small=wsm, w_uq=wuq, w_ukv=wukv,
                  w_out=woutT, w_gu=wguT, w_dn=wdnT)
    in_maps = []
    for core in range(8):
        if core < 6:
            xs = x_prompt[5 * core:5 * core + 5].reshape(T, D)
            cond = np.tile(f(c_ctx)[None, :], (NS, 1))
            samp = None
        else:
            b = core - 6
            xs = np.concatenate([x_prompt[30 + b], x_sample[b]], 0)
            cond = np.concatenate([f(c_ctx)[None, :], np.tile(f(c)[b][None, :], (4, 1))], 0)
            samp = b
        xT = np.ascontiguousarray(xs.T.reshape(KC, 128, T).transpose(1, 0, 2))
        condT = np.ascontiguousarray(cond.T.reshape(KC, 128, NS).transpose(1, 0, 2))
        tab = np.zeros((128, NTC), np.float32)
        mask = np.full((NS, 6), -30000.0, np.float32)
        ropeC = np.ones((64, T), np.float32)
        ropeS = np.zeros((64, T), np.float32)
        ctx_ckv = np.zeros((L, 128, 4, 256), np.float32)
        ctx_kr = np.zeros((L, 64, 256), np.float32)
        ginit = np.zeros((L, 2, 4, 128, 128), np.float32)
        sinit = np.zeros((L, 2, 8, 128, 64), np.float32)
        if samp is None:
            for s in range(NS):
                mask[s, s] = 0.0
        else:
            mask[0, 0] = 0.0
            mask[1:, 1:] = 0.0
            tab[:, TC_AL + 2:TC_AL + 8] = 1.0
            tab[:, TC_AR + 2:TC_AR + 8] = 1.0
            tab[:, TC_AS + 2:TC_AS + 5] = 1.0
            tab[:, TC_AS + 5 + 1:TC_AS + 5 + 4] = 1.0
            ropeC[:, SL:] = cos64
            ropeS[:, SL:] = sin64
            ctx_ckv = np.ascontiguousarray(f(cache_mla_ckv)[samp].transpose(0, 2, 1).reshape(L, 4, 128, 256).transpose(0, 2, 1, 3))
            ctx_kr = np.ascontiguousarray(f(cache_mla_krope)[samp].transpose(0, 2, 1))
            ginit = np.ascontiguousarray(f(state_gdn)[samp])
            sinit = np.ascontiguousarray(f(state_ssd)[samp].transpose(0, 1, 2, 4, 3))
        tab[:, TC_MASK:TC_MASK + 30] = mask.reshape(-1)[None, :]
        m = dict(shared)
        m.update(xT=xT, condT=condT, tab=tab, ropeC=ropeC, ropeS=ropeS, ctx_ckv=ctx_ckv, ctx_kr=ctx_kr,
                 gdn_init=ginit, ssd_init=sinit)
        in_maps.append(m)

    if "nc" not in _CACHE:
        _CACHE["nc"] = build_program()
    res = run_bass_kernel_spmd(_CACHE["nc"], in_maps, core_ids=list(range(8)))
    R = res.results
    _CACHE["R"] = R

    y_prompt = np.zeros((32, SL, D), np.float32)
    y_sample = np.zeros((2, 1024, D), np.float32)
    n_ckv = np.zeros((32, L, SL, 512), np.float32)
    n_kr = np.zeros((32, L, SL, 64), np.float32)
    n_gdn = np.zeros((32, L, 2, 4, 128, 128), np.float32)
    n_ssd = np.zeros((32, L, 2, 8, 64, 128), np.float32)
    for core in range(8):
        r = R[core]
        y = r["yT"].transpose(1, 0, 2).reshape(D, T).T
        ck = r["o_ckv"].transpose(0, 3, 2, 1).reshape(L, T, 512)
        kr = r["o_kr"].transpose(0, 2, 1)
        slots = [(s, 5 * core + s) for s in range(NS)] if core < 6 else [(0, 30 + core - 6)]
        for s, seq in slots:
            y_prompt[seq] = y[s * SL:(s + 1) * SL]
            n_ckv[seq] = ck[:, s * SL:(s + 1) * SL]
            n_kr[seq] = kr[:, s * SL:(s + 1) * SL]
            n_gdn[seq] = r["o_gdn"][:, s]
            n_ssd[seq] = r["o_ssd"][:, s].transpose(0, 1, 2, 4, 3)
        if core >= 6:
            y_sample[core - 6] = y[SL:]
    return (y_prompt, y_sample, n_ckv, n_kr, n_gdn, n_ssd)
```
